# Optimizing a Trainium2 kernel written in Bass

```python
import math
import jax, jax.numpy as jnp
from jax import lax
import numpy as np

D_MODEL = 2048
BATCH = 8
SEQ = 2048
DEPTH = 2

D_MIX = D_MODEL
A_HEADS = 4
A_HEAD_DIM = 128
A_QK_HALF = A_HEAD_DIM // 2
D_A = A_HEADS * A_HEAD_DIM
B_HEADS = 8
B_HEAD_DIM = 128
D_B = B_HEADS * B_HEAD_DIM
DILATED_PATTERNS = ((128, 1), (512, 4), (2048, 16))
D_C = D_MIX - D_A - D_B
CONV_WIDTH = 31

BLOCK = 128
N_BUCKETS = 32
MAX_DISTANCE = 2048
N_ATTN_HEADS = A_HEADS + B_HEADS
ALPHA = (2 * DEPTH) ** 0.25
BETA = (8 * DEPTH) ** -0.25
EPS = 1e-5

IN_SIZES = (D_A, D_A, D_A, D_A,
            D_B, D_B, D_B, D_B,
            D_C, D_C, D_C)
D_IN = sum(IN_SIZES)
IN_OFFSETS = [int(o) for o in np.cumsum(IN_SIZES)[:-1]]

kernel_name = "hymba_diff_dilated_conformer_deepnorm"


def layer_norm(x, g, b):
    x32 = x.astype(jnp.float32)
    mu = jnp.mean(x32, axis=-1, keepdims=True)
    var = jnp.mean(jnp.square(x32 - mu), axis=-1, keepdims=True)
    y = (x32 - mu) * lax.rsqrt(var + EPS) * g.astype(jnp.float32) + b.astype(jnp.float32)
    return y.astype(x.dtype)


def t5_bucket(dist):
    max_exact = N_BUCKETS // 2
    d = jnp.maximum(dist, 0)
    df = jnp.maximum(d, 1).astype(jnp.float32)
    large = max_exact + (jnp.log(df / max_exact) / math.log(MAX_DISTANCE / max_exact)
                         * (N_BUCKETS - max_exact)).astype(jnp.int32)
    large = jnp.minimum(large, N_BUCKETS - 1)
    return jnp.where(d < max_exact, d, large)


def diff_attention(q, k, v, bias_dist, lam, lam_init, head_gain):
    B, S, H, _, Dq = q.shape
    nb = S // BLOCK
    scale = Dq ** -0.5
    qb = jnp.moveaxis(q.reshape(B, nb, BLOCK, H, 2, Dq), 1, 0)
    k_pos = jnp.arange(S)

    def one_block(args):
        qi, i = args
        q_pos = i * BLOCK + jnp.arange(BLOCK)
        dist = q_pos[:, None] - k_pos[None, :]
        bias = bias_dist[jnp.clip(dist, 0, S - 1)].astype(jnp.float32)
        s = jnp.einsum('bqhcd,bkhcd->bhcqk', qi, k).astype(jnp.float32) * scale
        s = s + jnp.transpose(bias, (2, 0, 1))[None, :, None]
        s = jnp.where(dist >= 0, s, -jnp.inf)
        p = jax.nn.softmax(s, axis=-1)
        a = p[:, :, 0] - lam * p[:, :, 1]
        return jnp.einsum('bhqk,bkhd->bqhd', a.astype(v.dtype), v)

    out = lax.map(one_block, (qb, jnp.arange(nb)))
    out = jnp.moveaxis(out, 0, 1).reshape(B, S, H, v.shape[-1]).astype(jnp.float32)
    out = out * lax.rsqrt(jnp.mean(jnp.square(out), axis=-1, keepdims=True) + EPS)
    out = out * head_gain.astype(jnp.float32) * (1.0 - lam_init)
    return out.astype(v.dtype)


def _residue_view(x, dil):
    B, S = x.shape[:2]
    L = S // dil
    xr = jnp.moveaxis(x.reshape((B, L, dil) + x.shape[2:]), 2, 1).reshape((B * dil, L) + x.shape[2:])
    pad = (-L) % BLOCK
    return jnp.pad(xr, ((0, 0), (0, pad)) + ((0, 0),) * (x.ndim - 2))


def _residue_unview(y, B, dil, L):
    y = y[:, :L]
    y = jnp.moveaxis(y.reshape((B, dil) + y.shape[1:]), 1, 2)
    return y.reshape((B, dil * L) + y.shape[3:])


def dilated_attention(q, k, v, bias_table):
    B, S, H, D = q.shape
    scale = D ** -0.5
    qi = jnp.arange(BLOCK)
    kj = jnp.arange(2 * BLOCK)
    lag = BLOCK + qi[:, None] - kj[None, :]
    outs, maxes, sums = [], [], []
    for window, dil in DILATED_PATTERNS:
        L = S // dil
        qr, kr, vr = _residue_view(q, dil), _residue_view(k, dil), _residue_view(v, dil)
        N, Lp = qr.shape[:2]
        nb = Lp // BLOCK
        qb = qr.reshape(N, nb, BLOCK, H, D)
        kp = jnp.pad(kr, ((0, 0), (BLOCK, 0), (0, 0), (0, 0))).reshape(N, nb + 1, BLOCK, H, D)
        vp = jnp.pad(vr, ((0, 0), (BLOCK, 0), (0, 0), (0, 0))).reshape(N, nb + 1, BLOCK, H, D)
        kb = jnp.concatenate([kp[:, :-1], kp[:, 1:]], axis=2)
        vb = jnp.concatenate([vp[:, :-1], vp[:, 1:]], axis=2)
        key_idx = (jnp.arange(nb) * BLOCK - BLOCK)[:, None] + kj[None, :]
        valid = ((lag >= 0) & (lag <= window // dil))[None] & (key_idx >= 0)[:, None, :]
        bias = bias_table[t5_bucket(kj * dil)]
        bias_qk = bias[jnp.clip(lag, 0, 2 * BLOCK - 1)].astype(jnp.float32)
        s = jnp.einsum('nbqhd,nbkhd->nbhqk', qb, kb).astype(jnp.float32) * scale
        s = s + jnp.transpose(bias_qk, (2, 0, 1))[None, None]
        s = jnp.where(valid[None, :, None], s, -jnp.inf)
        m = jnp.max(s, axis=-1)
        p = jnp.exp(s - m[..., None])
        l = jnp.sum(p, axis=-1)
        o = jnp.einsum('nbhqk,nbkhd->nbqhd', p.astype(v.dtype), vb).astype(jnp.float32)
        l_t = jnp.swapaxes(l, 2, 3)
        o = o / l_t[..., None]
        outs.append(_residue_unview(o.reshape(N, Lp, H, D), B, dil, L))
        maxes.append(_residue_unview(jnp.swapaxes(m, 2, 3).reshape(N, Lp, H), B, dil, L))
        sums.append(_residue_unview(l_t.reshape(N, Lp, H), B, dil, L))
    m_all = jnp.stack(maxes)
    w = jnp.stack(sums) * jnp.exp(m_all - jnp.max(m_all, axis=0, keepdims=True))
    out = jnp.sum(w[..., None] * jnp.stack(outs), axis=0) / jnp.sum(w, axis=0)[..., None]
    return out.astype(q.dtype)


def conformer_conv(u, glu_gate, w_dw, b_dw, ln_g, ln_b, w_pw):
    h = u * jax.nn.sigmoid(glu_gate)
    h = lax.conv_general_dilated(h, w_dw[:, None, :], window_strides=(1,),
                                 padding=[(CONV_WIDTH - 1, 0)],
                                 dimension_numbers=('NWC', 'WIO', 'NWC'),
                                 feature_group_count=D_C) + b_dw
    h = jax.nn.silu(layer_norm(h, ln_g, ln_b))
    return h @ w_pw


def hybrid_layer(x, w_in, diff_lambda, diff_head_gain, conv_dw, conv_b, conv_ln_g, conv_ln_b,
                 conv_pw, w_out, ln_g, ln_b, rel_bias, lam_init):
    B, S, _ = x.shape
    proj = x @ w_in
    aq, ak, av, ag, bq, bk, bv, bg, cu, cglu, cg = jnp.split(proj, IN_OFFSETS, axis=-1)

    lam_v = diff_lambda.astype(jnp.float32)
    lam = (jnp.exp(jnp.sum(lam_v[0] * lam_v[1])) - jnp.exp(jnp.sum(lam_v[2] * lam_v[3]))
           + lam_init)
    bias_a = rel_bias[t5_bucket(jnp.arange(S))][:, :A_HEADS]
    ya = diff_attention(aq.reshape(B, S, A_HEADS, 2, A_QK_HALF),
                        ak.reshape(B, S, A_HEADS, 2, A_QK_HALF),
                        av.reshape(B, S, A_HEADS, A_HEAD_DIM),
                        bias_a, lam, lam_init, diff_head_gain).reshape(B, S, D_A)
    ya = ya * jax.nn.silu(ag)

    yb = dilated_attention(bq.reshape(B, S, B_HEADS, B_HEAD_DIM),
                           bk.reshape(B, S, B_HEADS, B_HEAD_DIM),
                           bv.reshape(B, S, B_HEADS, B_HEAD_DIM),
                           rel_bias[:, A_HEADS:]).reshape(B, S, D_B)
    yb = yb * jax.nn.silu(bg)

    yc = conformer_conv(cu, cglu, conv_dw, conv_b, conv_ln_g, conv_ln_b, conv_pw)
    yc = yc * jax.nn.silu(cg)

    y = jnp.concatenate([ya, yb, yc], axis=-1) @ w_out
    return layer_norm(ALPHA * x + y, ln_g, ln_b)


def setup_inputs(seed: int = 0) -> dict:
    key = jax.random.key(seed)
    ks = jax.random.split(key, 14)
    nrm = jax.random.normal
    return {
        "x": nrm(ks[0], (BATCH, SEQ, D_MODEL), jnp.float32),
        "w_in": nrm(ks[1], (DEPTH, D_MODEL, D_IN), jnp.float32) * D_MODEL ** -0.5,
        "diff_lambda": 0.1 * nrm(ks[2], (DEPTH, 4, A_QK_HALF), jnp.float32),
        "diff_head_gain": 1.0 + 0.02 * nrm(ks[3], (DEPTH, A_HEAD_DIM), jnp.float32),
        "conv_dw": nrm(ks[4], (DEPTH, CONV_WIDTH, D_C), jnp.float32) * CONV_WIDTH ** -0.5,
        "conv_b": 0.02 * nrm(ks[5], (DEPTH, D_C), jnp.float32),
        "conv_ln_g": 1.0 + 0.02 * nrm(ks[6], (DEPTH, D_C), jnp.float32),
        "conv_ln_b": 0.02 * nrm(ks[7], (DEPTH, D_C), jnp.float32),
        "conv_pw": nrm(ks[8], (DEPTH, D_C, D_C), jnp.float32) * D_C ** -0.5,
        "w_out": nrm(ks[9], (DEPTH, D_MIX, D_MODEL), jnp.float32) * (D_MIX ** -0.5 * BETA),
        "ln_g": 1.0 + 0.02 * nrm(ks[10], (DEPTH, D_MODEL), jnp.float32),
        "ln_b": 0.02 * nrm(ks[11], (DEPTH, D_MODEL), jnp.float32),
        "rel_bias": 0.5 * nrm(ks[12], (N_BUCKETS, N_ATTN_HEADS), jnp.float32),
    }


def reference(x, w_in, diff_lambda, diff_head_gain, conv_dw, conv_b, conv_ln_g, conv_ln_b,
              conv_pw, w_out, ln_g, ln_b, rel_bias):
    for layer in range(DEPTH):
        lam_init = 0.8 - 0.6 * math.exp(-0.3 * layer)
        x = hybrid_layer(x, w_in[layer], diff_lambda[layer], diff_head_gain[layer],
                         conv_dw[layer], conv_b[layer], conv_ln_g[layer], conv_ln_b[layer],
                         conv_pw[layer], w_out[layer], ln_g[layer], ln_b[layer], rel_bias,
                         lam_init)
    return x
```

```python
import math
from contextlib import ExitStack

import numpy as np
import concourse.bass as bass
import concourse.mybir as mybir
from concourse.bass_utils import run_bass_kernel_spmd

F32 = mybir.dt.float32
BF16 = mybir.dt.bfloat16
AF = mybir.ActivationFunctionType
ALU = mybir.AluOpType
AX = mybir.AxisListType

S = 2048
D = 2048
DIN = 7680
DEPTH = 2
MASK = -30000.0
EPS = 1e-5
ALPHA = (2 * DEPTH) ** 0.25
PATTERNS = ((128, 1), (512, 4), (2048, 16))


class _Op:
    __slots__ = ("eng", "fn", "deps", "dma_sem", "dma_val", "signals", "sig_no", "pos")

    def __init__(self, eng, fn):
        self.eng = eng
        self.fn = fn
        self.deps = []
        self.dma_sem = None
        self.dma_val = 0
        self.signals = False
        self.sig_no = 0
        self.pos = 0


class Sched:
    ENGS = ("pe", "act", "dve", "pool", "sp")

    def __init__(self, nc):
        self.nc = nc
        self.ops = []
        self.last_w = {}
        self.readers = {}
        self.dma_cnt = {}

    tags = ()

    def _add(self, eng, fn, reads, writes, arena=True):
        op = _Op(eng, fn)
        reads = list(reads)
        if arena:
            reads.extend(self.tags)
        deps = set()
        for r in reads:
            w = self.last_w.get(r)
            if w is not None:
                deps.add(w)
        for w_ in writes:
            rds = self.readers.get(w_, ())
            if rds:
                for rd in rds:
                    deps.add(rd)
            else:
                w = self.last_w.get(w_)
                if w is not None:
                    deps.add(w)
        idx = len(self.ops)
        op.deps = deps
        self.ops.append(op)
        for r in reads:
            self.readers.setdefault(r, []).append(idx)
        for w_ in writes:
            self.last_w[w_] = idx
            self.readers[w_] = []
        return op

    def op(self, eng, fn, reads=(), writes=()):
        return self._add(eng, fn, reads, writes)

    def dma(self, eng, slot, fn, reads=(), writes=()):
        op = self._add(eng, fn, reads, writes)
        self.dma_cnt[slot] = self.dma_cnt.get(slot, 0) + 1
        op.dma_sem = slot
        op.dma_val = 16 * self.dma_cnt[slot]
        return op

    def boundary(self, fn, tags):
        self._add("dve", fn, (), tuple(tags), arena=False)

    def emit(self, stack):
        nc = self.nc
        ops = self.ops
        per_eng = {e: [] for e in self.ENGS}
        for i, op in enumerate(ops):
            op.pos = len(per_eng[op.eng])
            per_eng[op.eng].append(i)
        waited = {e: {} for e in self.ENGS}
        need = [None] * len(ops)
        cur = {}
        for i, op in enumerate(ops):
            chans = {}
            for d in op.deps:
                p = ops[d]
                if p.dma_sem is not None:
                    ch = ("dma", p.dma_sem)
                    v = cur[p.dma_sem]
                else:
                    if p.eng == "pe" and op.eng == "pe":
                        continue
                    ch = ("eng", p.eng)
                    v = p.pos
                if ch not in chans or chans[ch][0] < v:
                    chans[ch] = (v, d)
            lst = []
            for ch, (v, d) in chans.items():
                prev = waited[op.eng].get(ch, -1)
                if prev >= v:
                    continue
                waited[op.eng][ch] = v
                lst.append((ch, v if ch[0] == "dma" else d))
                if ch[0] == "eng":
                    ops[d].signals = True
            need[i] = lst
            op.deps = None
            if op.dma_sem is not None:
                cur[op.dma_sem] = op.dma_val
        cnt = {e: 0 for e in self.ENGS}
        for op in ops:
            if op.dma_sem is None and op.signals:
                cnt[op.eng] += 1
                op.sig_no = cnt[op.eng]
        esem = {e: stack.enter_context(nc.semaphore("c_" + e)) for e in self.ENGS}
        dsem = {s: stack.enter_context(nc.semaphore("d_%d" % k)) for k, s in enumerate(self.dma_cnt)}
        dma_owner = {}
        for op in ops:
            if op.dma_sem is not None:
                dma_owner[op.dma_sem] = op.eng
        block = stack.enter_context(nc.Block())
        regs = {"pe": block.tensor, "act": block.scalar, "dve": block.vector,
                "pool": block.gpsimd, "sp": block.sync}

        def make_body(ename):
            def body(eng):
                for i in per_eng[ename]:
                    op = ops[i]
                    for ch, d in need[i]:
                        if ch[0] == "dma":
                            eng.wait_ge(dsem[ch[1]], d)
                        else:
                            eng.wait_ge(esem[ch[1]], ops[d].sig_no)
                    inst = op.fn(eng)
                    if op.dma_sem is not None:
                        inst.then_inc(dsem[op.dma_sem], 16)
                    elif op.signals:
                        inst.then_inc(esem[ename], 1)
                for s_, c in self.dma_cnt.items():
                    if dma_owner.get(s_) == ename:
                        eng.wait_ge(dsem[s_], 16 * c)
            return body

        for e in self.ENGS:
            regs[e](make_body(e))


def t5_bucket_np(d):
    d = np.maximum(np.asarray(d, dtype=np.int64), 0)
    df = np.maximum(d, 1).astype(np.float32)
    large = 16 + (np.log(df / np.float32(16)) / np.float32(math.log(2048 / 16)) * np.float32(16)).astype(np.int32)
    large = np.minimum(large, 31)
    return np.where(d < 16, d, large).astype(np.int64)


def build(n_layers=DEPTH, taps=()):
    nc = bass.Bass("TRN2", target_bir_lowering=False)

    def din(name, shape, dt=F32):
        return nc.dram_tensor(name, shape, dt, kind="ExternalInput")

    def dscr(name, shape, dt):
        return nc.dram_tensor(name, shape, dt, kind="Internal")

    x_d = din("x", [S, D])
    win_d = din("w_in", [DEPTH, D, DIN])
    wout_d = din("w_out", [DEPTH, D, D])
    wpw_d = din("conv_pw", [DEPTH, 512, 512])
    dlam_d = din("dlam", [DEPTH, 256])
    hgain_d = din("hgain", [DEPTH, 128])
    cpar_d = din("cpar", [DEPTH, 34, 512])
    lng_d = din("ln_g", [DEPTH, D])
    lnb_d = din("ln_b", [DEPTH, D])
    biasA_d = din("biasA", [4, 128, S])
    biasB_d = din("biasB", [3, 8, 128, 256])
    ident_d = din("ident", [128, 128])
    y_d = nc.dram_tensor("y", [S, D], F32, kind="ExternalOutput")

    xres_d = y_d
    qkA_d = dscr("qkA", [4, 2, 128, S], BF16)
    vA_d = dscr("vA", [S, 512], BF16)
    gA_d = dscr("gA", [S, 512], BF16)
    qkB_d = dscr("qkB", [8, 2, 128, S], BF16)
    vB_d = dscr("vB", [S, 1024], BF16)
    gBT_d = dscr("gBT", [1024, S], BF16)
    cuT_d = dscr("cuT", [512, S], F32)
    sgT_d = dscr("sgT", [512, S], F32)
    cgT_d = dscr("cgT", [512, S], BF16)
    wob_d = dscr("wob", [D, D], BF16)
    catT_d = dscr("catT", [D, S], BF16)
    tap_d = {}
    for t in taps:
        tap_d[t] = nc.dram_tensor("tap_" + t, [D, S], BF16, kind="ExternalOutput")

    def DAP(h, offset, pairs):
        return bass.AP(tensor=h, offset=offset, ap=[[int(a), int(b)] for a, b in pairs])

    ARENA_BYTES = 206 * 1024
    arena = nc.alloc_sbuf_tensor("arena", [128, ARENA_BYTES // 2], BF16)

    class Alloc:
        def __init__(self, base, limit):
            self.off = base
            self.limit = limit

        def get(self, dt, *free):
            n = 1
            for f in free:
                n *= f
            nbytes = n * (4 if dt == F32 else 2)
            off = self.off
            self.off = (off + nbytes + 63) // 64 * 64
            assert self.off <= self.limit, ("arena overflow", self.off, self.limit)
            v = arena[:, off // 2:(off + nbytes) // 2]
            if dt == F32:
                v = v.bitcast(F32)
            if len(free) == 2:
                v = v.rearrange("p (a b) -> p a b", b=free[1])
            elif len(free) == 3:
                v = v.rearrange("p (a b c) -> p a b c", b=free[1], c=free[2])
            return v

    pers = Alloc(0, ARENA_BYTES)
    big = pers.get(BF16, 16, S)
    identf = pers.get(F32, 128)
    identb = pers.get(BF16, 128)
    onesb = pers.get(BF16, 128)
    onesf = pers.get(F32, 128)
    neglam = pers.get(F32, 1)
    gainrep = pers.get(F32, 128)
    cp = pers.get(F32, 4, 34)
    epst = pers.get(F32, 1)
    junk = pers.get(F32, 16)
    R1_BASE = pers.off
    R1_BYTES = 58 * 1024
    R2_BASE = R1_BASE + R1_BYTES

    def A1():
        return Alloc(R1_BASE, R2_BASE)

    def A2():
        return Alloc(R2_BASE, ARENA_BYTES)

    def A12():
        return Alloc(R1_BASE, ARENA_BYTES)

    psall = nc.alloc_psum_tensor("psall", [128, 4096], F32)[:, :]
    ps = [psall[:, i * 512:(i + 1) * 512] for i in range(8)]
    psh = [psall[:, s_ * 256:(s_ + 1) * 256] for s_ in range(4)]

    def bc_inner(a2, m):
        return bass.AP(tensor=a2.tensor, offset=a2.offset, ap=[list(a2.ap[0]), list(a2.ap[1]), [0, m]])

    def bc_mid(a2, r):
        return bass.AP(tensor=a2.tensor, offset=a2.offset, ap=[list(a2.ap[0]), [0, r], list(a2.ap[1])])

    with ExitStack() as st:
        K = Sched(nc)
        TALL = ("R1", "R2")

        def MM(out, lhsT, rhs, start, stop, r, w):
            K.op("pe", lambda e: e.matmul(out, lhsT=lhsT, rhs=rhs, start=start, stop=stop), r, w)

        def TR(out, in_, ident, r, w):
            K.op("pe", lambda e: e.transpose(out, in_, ident), r, w)

        def ACTF(out, in_, func, r, w, scale=1.0, bias=0.0):
            K.op("act", lambda e: e.activation(out=out, in_=in_, func=func, bias=bias, scale=scale), r, w)

        def TT(eng, out, in0, in1, op, r, w):
            K.op(eng, lambda e: e.tensor_tensor(out=out, in0=in0, in1=in1, op=op), r, w)

        def TS(eng, out, in0, s1, s2, op0, op1, r, w):
            if s2 is None:
                K.op(eng, lambda e: e.tensor_scalar(out=out, in0=in0, scalar1=s1, scalar2=None, op0=op0), r, w)
            else:
                K.op(eng, lambda e: e.tensor_scalar(out=out, in0=in0, scalar1=s1, scalar2=s2, op0=op0, op1=op1), r, w)

        def STT(out, in0, scalar, in1, op0, op1, r, w):
            K.op("dve", lambda e: e.scalar_tensor_tensor(out=out, in0=in0, scalar=scalar, in1=in1, op0=op0, op1=op1), r, w)

        def CP(eng, out, in_, r, w):
            if eng == "act":
                K.op("act", lambda e: e.copy(out=out, in_=in_), r, w)
            else:
                K.op(eng, lambda e: e.tensor_copy(out=out, in_=in_), r, w)

        def RECIP(out, in_, r, w):
            K.op("dve", lambda e: e.reciprocal(out=out, in_=in_), r, w)

        def MEMSET(eng, ap, val, r, w):
            K.op(eng, lambda e: e.memset(ap, val), r, w)

        def DMA(eng, slot, out, in_, r, w):
            K.dma(eng, slot, lambda e: e.dma_start(out=out, in_=in_), r, w)

        def BOUNDARY(tags=TALL):
            K.boundary(lambda e: e.memset(junk[:, 0:8], 0.0), tags)

        def drain(gen, tags):
            K.tags = tags
            for _ in gen:
                pass

        K.tags = TALL
        DMA("sp", "c_id", identf, ident_d.ap(), [], ["identf"])
        CP("dve", identb, identf, ["identf"], ["identb"])
        MEMSET("dve", onesb, 1.0, [], ["onesb"])
        MEMSET("dve", onesf, 1.0, [], ["onesf"])
        MEMSET("dve", epst, EPS, [], ["epst"])

        bank_rr = [0]

        def next_bank(lo=0, hi=8):
            b = lo + bank_rr[0] % (hi - lo)
            bank_rr[0] += 1
            return b

        for layer in range(n_layers):
            lam_init = 0.8 - 0.6 * math.exp(-0.3 * layer)
            xin_d = x_d if layer == 0 else xres_d
            xout_d = y_d if layer == n_layers - 1 else xres_d

            BOUNDARY()
            K.tags = TALL
            al = A2()
            dl = al.get(F32, 256)
            tmpa = al.get(F32, 64)
            tmpb = al.get(F32, 64)
            s12 = al.get(F32, 2)
            e12 = al.get(F32, 2)
            cps = al.get(F32, 512)
            xin = [al.get(F32, D) for _ in range(4)]
            DMA("sp", "p_dl", dl, DAP(dlam_d, layer * 256, [(0, 128), (1, 256)]), [], ["dl"])
            TT("dve", tmpa, dl[:, 0:64], dl[:, 64:128], ALU.mult, ["dl"], ["tmpa"])
            TT("dve", tmpb, dl[:, 128:192], dl[:, 192:256], ALU.mult, ["dl"], ["tmpb"])
            K.op("dve", lambda e, o=s12[:, 0:1], i=tmpa: e.reduce_sum(out=o, in_=i, axis=AX.X), ["tmpa"], ["s1"])
            K.op("dve", lambda e, o=s12[:, 1:2], i=tmpb: e.reduce_sum(out=o, in_=i, axis=AX.X), ["tmpb"], ["s2"])
            ACTF(e12, s12, AF.Exp, ["s1", "s2"], ["e12"])
            TT("dve", tmpa[:, 0:1], e12[:, 0:1], e12[:, 1:2], ALU.subtract, ["e12"], ["lamv"])
            TS("dve", neglam, tmpa[:, 0:1], -1.0, -lam_init, ALU.mult, ALU.add, ["lamv"], ["neglam"])
            DMA("sp", "p_hg", gainrep, DAP(hgain_d, layer * 128, [(0, 128), (1, 128)]), [], ["gainraw"])
            TS("dve", gainrep, gainrep, 1.0 - lam_init, None, ALU.mult, None, ["gainraw"], ["gainrep"])
            DMA("sp", "p_cp", cps[0:34, :], cpar_d.ap()[layer], [], ["cps"])
            for cc in range(4):
                TR(ps[7][:, cc * 64:cc * 64 + 34], cps[0:34, cc * 128:(cc + 1) * 128], identf[0:34, 0:34],
                   ["cps", "identf"], [("ps", 7)])
            for cc in range(4):
                CP("dve", cp[:, cc, :], ps[7][:, cc * 64:cc * 64 + 34], [("ps", 7)], ["cp"])

            ev = 0
            for tt in (range(16) if layer == 0 else ()):
                xb = xin[tt % 4]
                DMA("sp" if tt % 2 == 0 else "act", "x0_%d" % (tt % 4), xb, xin_d.ap()[tt * 128:(tt + 1) * 128, :],
                    ["xres%d" % tt], [("xin", tt % 4)])
                for k4 in range(4):
                    bk = next_bank(0, 4)
                    for q in range(4):
                        kc = k4 * 4 + q
                        TR(ps[bk][:, q * 128:(q + 1) * 128], xb[:, kc * 128:(kc + 1) * 128], identf,
                           [("xin", tt % 4), "identf"], [("ps", bk)])
                    outv = big[:, k4 * 4:(k4 + 1) * 4, tt * 128:(tt + 1) * 128]
                    inv = ps[bk].rearrange("p (a b) -> p a b", b=128)
                    CP("act" if ev % 2 == 0 else "dve", outv, inv, [("ps", bk)],
                       [("big", k4 * 4 + q) for q in range(4)])
                    ev += 1
            for q_ in range(4):
                DMA("pool", "wobc", wob_d.ap()[q_ * 512:(q_ + 1) * 512, :],
                    wout_d.ap()[layer, q_ * 512:(q_ + 1) * 512, :], [], [("wob", q_)])
            BOUNDARY()

            al = A1()
            wst = [al.get(F32, 4, 512) for _ in range(2)]
            wb = [al.get(BF16, 16, 512) for _ in range(2)]
            stg = [al.get(F32, 512) for _ in range(4)]
            gstg = [al.get(BF16, 512) for _ in range(2)]
            stg_i = [0]
            gst_i = [0]
            wq_i = [0]

            def load_w(nblk, buf):
                for k4 in range(4):
                    q_ = wq_i[0] % 2
                    wq_i[0] += 1
                    src = DAP(win_d, layer * D * DIN + k4 * 4 * 128 * DIN + nblk * 512,
                              [(DIN, 128), (128 * DIN, 4), (1, 512)])
                    DMA("sp", "wst%d" % q_, wst[q_], src, [], [("wst", q_)])
                    for k_ in range(4):
                        kc = k4 * 4 + k_
                        CP("dve" if k_ % 2 == 0 else "pool", wb[buf][:, kc, :], wst[q_][:, k_, :],
                           [("wst", q_)], [("wb", buf, kc)])

            def store_stage(func, scale, dt, psb, bk, dst_ap, wkey, gain=False):
                si = stg_i[0] % 4
                stg_i[0] += 1
                sv = stg[si] if dt == F32 else stg[si].bitcast(BF16)[:, 0:512]
                if gain:
                    gi = gst_i[0] % 2
                    gst_i[0] += 1
                    ACTF(stg[si], psb, func, [("ps", bk)], [("st", si)], scale=scale)
                    TT("dve", gstg[gi].rearrange("p (a b) -> p a b", b=128),
                       stg[si].rearrange("p (a b) -> p a b", b=128), bc_mid(gainrep, 4), ALU.mult,
                       [("st", si), "gainrep"], [("gst", gi)])
                    DMA("act", "gst%d" % gi, dst_ap, gstg[gi], [("gst", gi)], [wkey])
                    return
                else:
                    ACTF(sv, psb, func, [("ps", bk)], [("st", si)], scale=scale)
                DMA("act", "st%d" % si, dst_ap, sv, [("st", si)], [wkey])

            BLK = {
                "aq": (0, AF.Copy, 0.125, True), "ak": (1, AF.Copy, 1.0, True),
                "av": (2, AF.Copy, 1.0, False), "ag": (3, AF.Silu, 1.0, False),
                "bq0": (4, AF.Copy, 128 ** -0.5, True), "bq1": (5, AF.Copy, 128 ** -0.5, True),
                "bk0": (6, AF.Copy, 1.0, True), "bk1": (7, AF.Copy, 1.0, True),
                "bv0": (8, AF.Copy, 1.0, False), "bv1": (9, AF.Copy, 1.0, False),
                "bg0": (10, AF.Silu, 1.0, True), "bg1": (11, AF.Silu, 1.0, True),
                "cu": (12, AF.Copy, 1.0, True), "cglu": (13, AF.Sigmoid, 1.0, True), "cg": (14, AF.Silu, 1.0, True),
            }
            ORDER = ["aq", "ak", "av", "ag", "bq0", "bk0", "bv0", "bg0", "bq1", "bk1", "bv1", "bg1", "cu", "cglu", "cg"]
            pj_state = {"i": 0}

            def gen_proj(count, banks=(6,)):
                i0 = pj_state["i"]
                if i0 == 0:
                    load_w(BLK[ORDER[0]][0], 0)
                for i_ in range(i0, i0 + count):
                    nm = ORDER[i_]
                    n, func, scale, fmaj = BLK[nm]
                    buf = i_ % 2
                    if i_ + 1 < len(ORDER):
                        load_w(BLK[ORDER[i_ + 1]][0], (i_ + 1) % 2)
                    for grp in range(16):
                        bk = banks[grp % len(banks)]
                        if fmaj:
                            fs, tb = grp // 4, grp % 4
                            for kc in range(16):
                                MM(ps[bk], wb[buf][:, kc, fs * 128:(fs + 1) * 128], big[:, kc, tb * 512:(tb + 1) * 512],
                                   kc == 0, kc == 15, [("wb", buf, kc), ("big", kc)], [("ps", bk)])
                                if kc % 4 == 3:
                                    yield
                            tsl = slice(tb * 512, (tb + 1) * 512)
                            if nm in ("aq", "ak"):
                                dst = qkA_d.ap()[fs, 0 if nm == "aq" else 1, :, tsl]
                                key = ("qkA", fs, nm, tb)
                                dt = BF16
                            elif nm[:2] in ("bq", "bk"):
                                hh = int(nm[2]) * 4 + fs
                                dst = qkB_d.ap()[hh, 0 if nm[:2] == "bq" else 1, :, tsl]
                                key = ("qkB", hh, nm[:2], tb)
                                dt = BF16
                            elif nm[:2] == "bg":
                                hh = int(nm[2]) * 4 + fs
                                dst = gBT_d.ap()[hh * 128:(hh + 1) * 128, tsl]
                                key = ("gBT", hh, tb)
                                dt = BF16
                            elif nm == "cu":
                                dst = cuT_d.ap()[fs * 128:(fs + 1) * 128, tsl]
                                key = ("cuT", fs, tb)
                                dt = F32
                            elif nm == "cglu":
                                dst = sgT_d.ap()[fs * 128:(fs + 1) * 128, tsl]
                                key = ("sgT", fs, tb)
                                dt = F32
                            else:
                                dst = cgT_d.ap()[fs * 128:(fs + 1) * 128, tsl]
                                key = ("cgT", fs, tb)
                                dt = BF16
                            store_stage(func, scale, dt, ps[bk], bk, dst, key)
                        else:
                            tt = grp
                            for kc in range(16):
                                MM(ps[bk], big[:, kc, tt * 128:(tt + 1) * 128], wb[buf][:, kc, :],
                                   kc == 0, kc == 15, [("wb", buf, kc), ("big", kc)], [("ps", bk)])
                                if kc % 4 == 3:
                                    yield
                            rsl = slice(tt * 128, (tt + 1) * 128)
                            if nm == "av":
                                dst = vA_d.ap()[rsl, :]
                                key = ("vA", tt)
                            elif nm == "ag":
                                dst = gA_d.ap()[rsl, :]
                                key = ("gA", tt)
                            else:
                                half = int(nm[2])
                                dst = vB_d.ap()[rsl, half * 512:(half + 1) * 512]
                                key = ("vB", half, tt)
                            store_stage(func, scale, BF16, ps[bk], bk, dst, key, gain=(nm == "ag"))
                pj_state["i"] = i0 + count

            def gen_A():
                al = A2()
                QT = [[al.get(BF16, S) for _ in range(2)] for _ in range(2)]
                KT = [al.get(BF16, S) for _ in range(2)]
                VA = [al.get(BF16, 16, 130) for _ in range(2)]
                RA = [al.get(F32, S) for _ in range(2)]
                GA = [al.get(BF16, 16, 128) for _ in range(2)]
                ssb = [al.get(F32, 256) for _ in range(3)]
                NPT = 6
                pT = [al.get(BF16, 256) for _ in range(6)]
                on0 = [al.get(F32, 2, 128) for _ in range(2)]
                rr = [al.get(F32, 16) for _ in range(2)]
                wa = [al.get(F32, 2, 128) for _ in range(2)]
                wsq = [al.get(F32, 2, 128) for _ in range(2)]
                wu = [al.get(F32, 2, 128) for _ in range(2)]
                wub = [al.get(BF16, 2, 128) for _ in range(2)]
                ost = [al.get(BF16, 256) for _ in range(2)]
                for b_ in range(2):
                    MEMSET("pool", VA[b_][:, :, 128:130], 1.0, [], [("VA", b_)])
                    MEMSET("pool", QT[b_][0][64:128, :], 0.0, [], [("QTz", b_, 0)])
                    MEMSET("pool", QT[b_][1][0:64, :], 0.0, [], [("QTz", b_, 1)])
                psT = ps[7].bitcast(BF16)
                pshA = [(0, ps[0]), (1, ps[1]), (6, ps[6])]
                LAG = 3

                def A_loads(h):
                    buf = h % 2
                    for c_ in range(2):
                        DMA("sp", "aQ%d%d" % (buf, c_), QT[buf][c_][c_ * 64:(c_ + 1) * 64, :],
                            qkA_d.ap()[h, 0, c_ * 64:(c_ + 1) * 64, :],
                            [("qkA", h, "aq", tb) for tb in range(4)], [("QT", buf, c_)])
                    DMA("sp", "aK%d" % buf, KT[buf], qkA_d.ap()[h, 1],
                        [("qkA", h, "ak", tb) for tb in range(4)], [("KT", buf)])
                    DMA("sp", "aV%d" % buf, VA[buf][:, :, 0:128],
                        DAP(vA_d, h * 128, [(512, 128), (128 * 512, 16), (1, 128)]),
                        [("vA", tt) for tt in range(16)], [("VA", buf)])
                    DMA("sp", "aR%d" % buf, RA[buf], biasA_d.ap()[h], [], [("RA", buf)])
                    DMA("sp", "aG%d" % buf, GA[buf],
                        DAP(gA_d, h * 128, [(512, 128), (128 * 512, 16), (1, 128)]),
                        [("gA", tt) for tt in range(16)], [("GA", buf)])

                stepsA = []
                gcount = 0
                for h in range(4):
                    for gg in range(8):
                        for c in range(2):
                            for j in range(2 * gg + 2):
                                stepsA.append((h, gg, c, j, gcount))
                            gcount += 1
                infoA = {}
                deferred = []

                def A_front(idx):
                    h, gg, c, j, gc = stepsA[idx]
                    buf = h % 2
                    a = max(0, j - 2 * gg)
                    q0 = gg * 256 + a * 128
                    N = 256 - a * 128
                    sbk, sap = pshA[idx % len(pshA)]
                    pb_ = idx % NPT
                    MM(sap[:, 0:N], KT[buf][:, j * 128:(j + 1) * 128],
                       QT[buf][c][:, q0:q0 + N], True, True,
                       [("KT", buf), ("QT", buf, c), ("QTz", buf, c)], [("ps", sbk)])
                    si3 = idx % 3
                    TT("dve", ssb[si3][:, 0:N], sap[:, 0:N], RA[buf][:, q0 - j * 128:q0 - j * 128 + N],
                       ALU.add, [("ps", sbk), ("RA", buf)], [("ssb", si3)])
                    ACTF(pT[pb_][:, 0:N], ssb[si3][:, 0:N], AF.Exp, [("ssb", si3)], [("pT", pb_)])
                    infoA[idx] = (a, pb_)

                def A_back(idx):
                    h, gg, c, j, gc = stepsA[idx]
                    buf = h % 2
                    a, pb_ = infoA[idx]
                    par = gc % 2
                    for i in range(2 * gg + a, 2 * gg + 2):
                        off = (i - 2 * gg - a) * 128
                        bk = 2 + 2 * par + (i - 2 * gg)
                        MM(ps[bk][:, 0:129], pT[pb_][:, off:off + 128], VA[buf][:, j, 0:129],
                           j == 0, j == i, [("pT", pb_), ("VA", buf)], [("ps", bk)])
                    if j == 2 * gg + 1:
                        A_post(h, gg, c, gc)

                def A_post(h, gg, c, gc):
                    buf = h % 2
                    par = gc % 2
                    b0 = 2 + 2 * par
                    acc3 = psall[:, b0 * 512:(b0 + 2) * 512].rearrange("p (a b) -> p a b", b=512)
                    accO = acc3[:, :, 0:128]
                    accl = acc3[:, :, 128:129]
                    pk = [("ps", b0), ("ps", b0 + 1)]
                    w_ = (gc // 2) % 2
                    R_ = rr[w_]
                    rk = ("rr", w_)
                    if c == 0:
                        RECIP(R_[:, 0:2].rearrange("p (a b) -> p a b", b=1), accl, pk, [(rk, 0)])
                        for a_ in range(2):
                            ACTF(on0[w_][:, a_, :], ps[b0 + a_][:, 0:128], AF.Copy, [("ps", b0 + a_), (rk, 0)],
                                 [("on0", w_, a_)], scale=R_[:, a_:a_ + 1])
                        return

                    def st1():
                        RECIP(R_[:, 2:4].rearrange("p (a b) -> p a b", b=1), accl, pk, [(rk, 1)])
                        TS("dve", R_[:, 4:6], R_[:, 2:4], neglam, None, ALU.mult, None, [(rk, 1), "neglam"], [(rk, 2)])
                        TT("dve", wa[w_], accO, bc_inner(R_[:, 4:6], 128), ALU.mult, pk + [(rk, 2)], [("wa", w_)])
                        TT("dve", wa[w_], wa[w_], on0[w_], ALU.add,
                           [("wa", w_), ("on0", w_, 0), ("on0", w_, 1)], [("wa", w_)])

                    def st2():
                        ACTF(wsq[w_], wa[w_], AF.Square, [("wa", w_)], [("wsq", w_)])
                        K.op("dve", lambda e, o=R_[:, 6:8], i_=wsq[w_]: e.reduce_sum(out=o, in_=i_, axis=AX.X),
                             [("wsq", w_)], [(rk, 3)])

                    def st3():
                        ACTF(R_[:, 8:10], R_[:, 6:8], AF.Ln, [(rk, 3), "epst"], [(rk, 4)], scale=1.0 / 128.0, bias=epst)
                        ACTF(R_[:, 10:12], R_[:, 8:10], AF.Exp, [(rk, 4)], [(rk, 5)], scale=-0.5)
                        for a_ in range(2):
                            ACTF(wu[w_][:, a_, :], wa[w_][:, a_, :], AF.Copy, [("wa", w_), (rk, 5)], [("wu", w_, a_)],
                                 scale=R_[:, 10 + a_:11 + a_])

                    def st4():
                        TT("dve", wub[w_], wu[w_], GA[buf][:, 2 * gg:2 * gg + 2, :], ALU.mult,
                           [("wu", w_, 0), ("wu", w_, 1), ("GA", buf)], [("wub", w_)])

                    def st5():
                        for a_ in range(2):
                            TR(psT[:, w_ * 256 + a_ * 128:w_ * 256 + (a_ + 1) * 128], wub[w_][:, a_, :], identb,
                               [("wub", w_), "identb"], [("ps", 7)])
                        CP("act", ost[w_], psT[:, w_ * 256:(w_ + 1) * 256], [("ps", 7)], [("ost", w_)])
                        DMA("pool", "aO%d" % w_, catT_d.ap()[h * 128:(h + 1) * 128, gg * 256:(gg + 1) * 256], ost[w_],
                            [("ost", w_)], [("catT", h, gg)])

                    pid = gc // 2
                    while deferred and deferred[0][0] <= pid - 2:
                        f_ = deferred.pop(0)[1]
                        if f_ is not None:
                            f_()
                    st1()
                    deferred.extend([(pid, st2), (pid, st3), (pid, st4)] + [(pid, None)] * 5 + [(pid, st5)])

                A_loads(0)
                A_loads(1)
                nA = len(stepsA)
                for idx in range(nA):
                    A_front(idx)
                    if idx >= LAG:
                        A_back(idx - LAG)
                        hp = stepsA[idx - LAG][0]
                        if stepsA[idx - LAG + 1][0] != hp and hp + 2 < 4:
                            while deferred:
                                f_ = deferred.pop(0)[1]
                                if f_ is not None:
                                    f_()
                            A_loads(hp + 2)
                    if deferred:
                        f_ = deferred.pop(0)[1]
                        if f_ is not None:
                            f_()
                    yield
                for k_ in range(LAG, 0, -1):
                    A_back(nA - k_)
                while deferred:
                    f_ = deferred.pop(0)[1]
                    if f_ is not None:
                        f_()

            def gen_B():
                al = A12()
                QN = [al.get(BF16, S) for _ in range(2)]
                KN = [al.get(BF16, S) for _ in range(2)]
                VP = [al.get(BF16, 3, 16, 128) for _ in range(2)]
                B2 = [al.get(F32, 3, 256) for _ in range(2)]
                GT = [al.get(BF16, S) for _ in range(2)]
                PT = [al.get(BF16, 48, 256) for _ in range(2)]
                QP = [[al.get(BF16, S) for _ in range(2)] for _ in range(2)]
                ssB = [al.get(F32, 256) for _ in range(3)]
                rlb = [al.get(F32, 512) for _ in range(2)]
                o32 = [al.get(F32, 512) for _ in range(2)]
                bst = [al.get(BF16, S) for _ in range(2)]
                SBK = (0, 1, 6)

                def B_loads(h):
                    buf = h % 2
                    DMA("sp", "bQ%d" % buf, QN[buf], qkB_d.ap()[h, 0],
                        [("qkB", h, "bq", tb) for tb in range(4)], [("QN", buf)])
                    DMA("sp", "bK%d" % buf, KN[buf], qkB_d.ap()[h, 1],
                        [("qkB", h, "bk", tb) for tb in range(4)], [("KN", buf)])
                    for p, (win, dil) in enumerate(PATTERNS):
                        Lp = S // dil
                        nb = Lp // 128
                        for r in range(dil):
                            src = DAP(vB_d, r * 1024 + h * 128, [(dil * 1024, 128), (128 * dil * 1024, nb), (1, 128)])
                            last = (p == 2 and r == dil - 1)
                            DMA("sp", "bV%d" % buf, VP[buf][:, p, r * nb:(r + 1) * nb, :], src,
                                [("vB", h // 4, tt) for tt in range(16)],
                                [("VP", buf, p, r)] + ([("VPall", buf)] if last else []))
                    DMA("sp", "bB%d" % buf, B2[buf], DAP(biasB_d, h * 128 * 256, [(256, 128), (8 * 128 * 256, 3), (1, 256)]),
                        [], [("B2", buf)])
                    DMA("sp", "bG%d" % buf, GT[buf], gBT_d.ap()[h * 128:(h + 1) * 128, :],
                        [("gBT", h, tb) for tb in range(4)], [("GT", buf)])
                    for pi, dil in ((0, 4), (1, 16)):
                        CP("pool", QP[buf][pi].rearrange("p (r l) -> p r l", r=dil),
                           QN[buf].rearrange("p (l r) -> p r l", r=dil), [("QN", buf)], [("QP", buf, pi)])

                def tid_p0(kb):
                    return 16 + (kb // 4) * 8 + kb % 4

                def tid_p1(r, b):
                    return 16 + b * 8 + 4 + r

                fronts = []
                group_last = {}
                for h in range(8):
                    for r16 in range(16):
                        fronts.append((h, r16, 2, r16, 16, 128))
                    for TB in range(4):
                        for kb in range(4 * TB, 4 * TB + 4):
                            fronts.append((h, tid_p0(kb), 0, kb * 128, 1, 256 if kb < 15 else 128))
                        for r in range(4):
                            fronts.append((h, tid_p1(r, TB), 1, TB * 512 + r, 4, 256 if TB < 3 else 128))
                        group_last[(h, TB)] = len(fronts) - 1
                fctr = [0]

                def B_front(fi):
                    h, tid, p, t0, dil, nq = fronts[fi]
                    buf = h % 2
                    sbk = SBK[fi % 3]
                    si3 = fi % 3
                    kap = KN[buf][:, t0:t0 + 127 * dil + 1:dil]
                    if p == 0:
                        qap = QN[buf][:, t0:t0 + nq]
                        qk = ("QN", buf)
                    else:
                        r_, l0 = t0 % dil, t0 // dil
                        qap = QP[buf][p - 1][:, r_ * (S // dil) + l0:r_ * (S // dil) + l0 + nq]
                        qk = ("QP", buf, p - 1)
                    MM(ps[sbk][:, 0:nq], kap, qap, True, True, [qk, ("KN", buf)], [("ps", sbk)])
                    TT("dve", ssB[si3][:, 0:nq], ps[sbk][:, 0:nq], B2[buf][:, p, 0:nq], ALU.add,
                       [("ps", sbk), ("B2", buf)], [("ssB", si3)])
                    ACTF(PT[buf][:, tid, 0:nq], ssB[si3][:, 0:nq], AF.Exp, [("ssB", si3)], [("PT", buf, tid)])

                blk_ctr = [0]

                def B_block(h, TB):
                    buf = h % 2
                    par = blk_ctr[0] % 2
                    blk_ctr[0] += 1
                    bO, bL = 2 + par, 4 + par
                    contrib = []
                    for qb in range(4 * TB, 4 * TB + 4):
                        oc = ((qb - 4 * TB) * 128, 1, 128)
                        if qb - 1 >= 0:
                            contrib.append((tid_p0(qb - 1), 128, 128, oc, VP[buf][:, 0, qb - 1, :], ("VP", buf, 0, 0)))
                        contrib.append((tid_p0(qb), 0, 128, oc, VP[buf][:, 0, qb, :], ("VP", buf, 0, 0)))
                    for r in range(4):
                        oc = (r, 4, 128)
                        if TB - 1 >= 0:
                            contrib.append((tid_p1(r, TB - 1), 128, 128, oc, VP[buf][:, 1, r * 4 + TB - 1, :], ("VP", buf, 1, r)))
                        contrib.append((tid_p1(r, TB), 0, 128, oc, VP[buf][:, 1, r * 4 + TB, :], ("VP", buf, 1, r)))
                    for r16 in range(16):
                        contrib.append((r16, 32 * TB, 32, (r16, 16, 32), VP[buf][:, 2, r16, :], ("VP", buf, 2, r16)))
                    n = len(contrib)

                    def mk(k_, tid, c0, ncol, o0, ostep, ocnt, vt, vkey):
                        def f():
                            rhs = PT[buf][:, tid, c0:c0 + ncol]
                            osl = slice(o0, o0 + (ocnt - 1) * ostep + 1, ostep)
                            MM(ps[bO][:, osl], vt, rhs, k_ == 0, k_ == n - 1,
                               [vkey, ("VPall", buf), ("PT", buf, tid)], [("ps", bO)])
                            MM(ps[bL][:, osl], onesb, rhs, k_ == 0, k_ == n - 1,
                               ["onesb", ("PT", buf, tid)], [("ps", bL)])
                        return f

                    for k_, (tid, c0, ncol, (o0, ostep, ocnt), vt, vkey) in enumerate(contrib):
                        pvq.append(mk(k_, tid, c0, ncol, o0, ostep, ocnt, vt, vkey))

                    def fin():
                        w_ = par
                        tsl = slice(TB * 512, (TB + 1) * 512)
                        ACTF(rlb[w_], ps[bL], AF.Ln, [("ps", bL)], [("rlb", w_)])
                        ACTF(rlb[w_], rlb[w_], AF.Exp, [("rlb", w_)], [("rlb", w_)], scale=-1.0)
                        TT("dve", o32[w_], ps[bO], rlb[w_], ALU.mult, [("ps", bO), ("rlb", w_)], [("o32", w_)])
                        TT("pool", bst[buf][:, tsl], o32[w_], GT[buf][:, tsl], ALU.mult,
                           [("o32", w_), ("GT", buf)], [("bst", buf, TB)])
                        if TB == 3:
                            DMA("pool", "bO%d" % buf, catT_d.ap()[(4 + h) * 128:(5 + h) * 128, :], bst[buf],
                                [("bst", buf, t_) for t_ in range(4)], [("catT", 4 + h)])
                            if h + 2 < 8:
                                B_loads(h + 2)
                    pvq.append(fin)

                LAGF = 4
                pvq = []
                pending = []
                for h in range(8):
                    for TB in range(4):
                        pending.append((group_last[(h, TB)], h, TB))
                B_loads(0)
                B_loads(1)
                for fi in range(len(fronts)):
                    B_front(fi)
                    while pending and pending[0][0] + LAGF <= fi:
                        _, h_, TB_ = pending.pop(0)
                        B_block(h_, TB_)
                    for _k in range(3):
                        if pvq:
                            pvq.pop(0)()
                    yield
                while pending:
                    _, h_, TB_ = pending.pop(0)
                    B_block(h_, TB_)
                while pvq:
                    pvq.pop(0)()

            drain(gen_proj(15, (4, 5, 6, 7)), ("R1",))
            BOUNDARY(("R2",))
            drain(gen_A(), ("R2",))
            BOUNDARY()
            drain(gen_B(), ("R2",))
            BOUNDARY()
            K.tags = TALL

            for hh in range(12):
                DMA("pool", "catl", big[:, hh, :], catT_d.ap()[hh * 128:(hh + 1) * 128, :], ([("catT", hh, g_) for g_ in range(8)] if hh < 4 else [("catT", hh)]), [("big", hh)])
            al = A12()
            wpst = al.get(F32, 4, 512)
            wpw = al.get(BF16, 4, 512)
            Dg = al.get(BF16, 4, 31, 128)
            cu = al.get(F32, S)
            sg = al.get(F32, S)
            hpad = al.get(BF16, 4, S + 32)
            y1 = al.get(F32, 4, S)
            ysq = wpst
            mean = al.get(F32, 512)
            msq = al.get(F32, 512)
            rstd = al.get(F32, 512)
            hn = [al.get(F32, 512) for _ in range(2)]
            hs = al.get(BF16, 4, 512)
            cgt = [al.get(BF16, 512) for _ in range(2)]
            DMA("sp", "c_w", wpst, DAP(wpw_d, layer * 512 * 512, [(512, 128), (128 * 512, 4), (1, 512)]), [], ["wpst"])
            CP("pool", wpw, wpst, ["wpst"], ["wpw"])
            for cc in range(4):
                for j in range(31):
                    if j % 2 == 0:
                        TS("dve", Dg[:, cc, j, :], identb, cp[:, cc, j:j + 1], None, ALU.mult, None,
                           ["identb", "cp"], [("Dg", cc, j)])
                    else:
                        ACTF(Dg[:, cc, j, :], identb, AF.Copy, ["identb", "cp"], [("Dg", cc, j)],
                             scale=cp[:, cc, j:j + 1])
            for cc in range(4):
                DMA("sp", "c_cu", cu, cuT_d.ap()[cc * 128:(cc + 1) * 128, :], [("cuT", cc, tb) for tb in range(4)], ["cu"])
                DMA("act", "c_sg", sg, sgT_d.ap()[cc * 128:(cc + 1) * 128, :], [("sgT", cc, tb) for tb in range(4)], ["sg"])
                MEMSET("pool", hpad[:, cc, 0:30], 0.0, [], [("hpad0", cc)])
                TT("dve", hpad[:, cc, 30:30 + S], cu, sg, ALU.mult, ["cu", "sg"], [("hpad", cc)])
            for cc in range(4):
                for tb in range(4):
                    bk = next_bank(0, 4)
                    for j in range(31):
                        MM(ps[bk], Dg[:, cc, j, :], hpad[:, cc, tb * 512 + j:tb * 512 + j + 512], j == 0, j == 30,
                           [("Dg", cc, j), ("hpad", cc), ("hpad0", cc)], [("ps", bk)])
                    ACTF(y1[:, cc, tb * 512:(tb + 1) * 512], ps[bk], AF.Identity, [("ps", bk), "cp"], [("y1", cc, tb)],
                         bias=cp[:, cc, 31:32])
            for tb in range(4):
                tsl = slice(tb * 512, (tb + 1) * 512)
                for cc in range(4):
                    ACTF(ysq[:, cc, :], y1[:, cc, tsl], AF.Square, [("y1", cc, tb)], [("ysq", cc), "wpst"])
                for cc in range(4):
                    MM(ps[4], onesf, y1[:, cc, tsl], cc == 0, cc == 3, ["onesf", ("y1", cc, tb)], [("ps", 4)])
                for cc in range(4):
                    MM(ps[5], onesf, ysq[:, cc, :], cc == 0, cc == 3, ["onesf", ("ysq", cc)], [("ps", 5)])
                ACTF(mean, ps[4], AF.Copy, [("ps", 4)], ["mean"], scale=1.0 / 512.0)
                TT("dve", msq, mean, mean, ALU.mult, ["mean"], ["msq"])
                STT(rstd, ps[5], 1.0 / 512.0, msq, ALU.mult, ALU.subtract, [("ps", 5), "msq"], ["rstd"])
                ACTF(rstd, rstd, AF.Ln, ["rstd", "epst"], ["rstd"], bias=epst)
                ACTF(rstd, rstd, AF.Exp, ["rstd"], ["rstd"], scale=-0.5)
                for cc in range(4):
                    hb = hn[cc % 2]
                    TT("dve", hb, y1[:, cc, tsl], mean, ALU.subtract, [("y1", cc, tb), "mean"], [("hn", cc % 2)])
                    TT("dve", hb, hb, rstd, ALU.mult, [("hn", cc % 2), "rstd"], [("hn", cc % 2)])
                    ACTF(hs[:, cc, :], hb, AF.Silu, [("hn", cc % 2), "cp"], [("hs", cc)],
                         scale=cp[:, cc, 32:33], bias=cp[:, cc, 33:34])
                for co in range(4):
                    bk = next_bank(0, 4)
                    gb = cgt[co % 2]
                    DMA("sp", "c_g%d" % (co % 2), gb, cgT_d.ap()[co * 128:(co + 1) * 128, tsl],
                        [("cgT", co, tb)], [("cgt", co % 2)])
                    for cc in range(4):
                        MM(ps[bk], wpw[:, cc, co * 128:(co + 1) * 128], hs[:, cc, :], cc == 0, cc == 3,
                           ["wpw", ("hs", cc)], [("ps", bk)])
                    TT("dve", big[:, 12 + co, tsl], ps[bk], gb, ALU.mult, [("ps", bk), ("cgt", co % 2)],
                       [("big", 12 + co)])

            BOUNDARY()
            if ("cat%d" % layer) in tap_d:
                DMA("sp", "tapcat", DAP(tap_d["cat%d" % layer], 0, [(S, 128), (128 * S, 16), (1, S)]), big,
                    [("big", kc) for kc in range(16)], ["tapcat"])
                BOUNDARY()
            al = A12()
            wo = al.get(BF16, 16, D)
            zt = [al.get(F32, D) for _ in range(2)]
            xrs = [al.get(F32, D) for _ in range(2)]
            grep = al.get(F32, D)
            brep = al.get(F32, D)
            stats = al.get(F32, 24)
            mv = al.get(F32, 4)
            DMA("act", "o_g", grep, DAP(lng_d, layer * D, [(0, 128), (1, D)]), [], ["grep"])
            DMA("act", "o_b", brep, DAP(lnb_d, layer * D, [(0, 128), (1, D)]), [], ["brep"])
            for n in range(4):
                DMA("sp", "wo4_%d" % n, wo[:, :, n * 512:(n + 1) * 512],
                    DAP(wob_d, n * 512, [(D, 128), (128 * D, 16), (1, 512)]),
                    [("wob", q_) for q_ in range(4)], [("wo", kc, n) for kc in range(16)])

            def load_x(tt):
                DMA("sp", "o_x%d" % (tt % 2), xrs[tt % 2], xin_d.ap()[tt * 128:(tt + 1) * 128, :],
                    ["xres%d" % tt], [("xr", tt % 2)])

            def next_xT(tt_):
                zb_ = zt[tt_ % 2]
                Zq_ = [("z", tt_ % 2, n_) for n_ in range(4)]
                for k4 in range(4):
                    bk_ = next_bank(0, 8)
                    for q in range(4):
                        kc_ = k4 * 4 + q
                        TR(ps[bk_][:, q * 128:(q + 1) * 128], zb_[:, kc_ * 128:(kc_ + 1) * 128], identf,
                           Zq_ + ["identf"], [("ps", bk_)])
                    CP("act", big[:, k4 * 4:(k4 + 1) * 4, tt_ * 128:(tt_ + 1) * 128],
                       ps[bk_].rearrange("p (a b) -> p a b", b=128), [("ps", bk_)], [("bigx", k4, tt_)])

            fuse_next = layer + 1 < n_layers
            load_x(0)
            for tt in range(16):
                zb = zt[tt % 2]
                xr = xrs[tt % 2]
                rsl = slice(tt * 128, (tt + 1) * 128)
                if tt + 1 < 16:
                    load_x(tt + 1)
                for n in range(4):
                    bk = next_bank(0, 8)
                    for kc in range(16):
                        MM(ps[bk], big[:, kc, rsl], wo[:, kc, n * 512:(n + 1) * 512], kc == 0, kc == 15,
                           [("big", kc), ("wo", kc, n)], [("ps", bk)])
                    STT(zb[:, n * 512:(n + 1) * 512], xr[:, n * 512:(n + 1) * 512], ALPHA, ps[bk], ALU.mult, ALU.add,
                        [("xr", tt % 2), ("ps", bk)], [("z", tt % 2, n)])
                if fuse_next and tt >= 1:
                    next_xT(tt - 1)
                for n in range(4):
                    K.op("dve", lambda e, o=stats[:, n * 6:(n + 1) * 6], i_=zb[:, n * 512:(n + 1) * 512]: e.bn_stats(out=o, in_=i_),
                         [("z", tt % 2, n)], [("stats", n)])
                K.op("dve", lambda e, o=mv[:, 0:2], i_=stats: e.bn_aggr(out=o, in_=i_),
                     [("stats", n) for n in range(4)], ["mv"])
                ACTF(mv[:, 2:3], mv[:, 1:2], AF.Sqrt, ["mv", "epst"], ["mv2"], bias=epst)
                RECIP(mv[:, 3:4], mv[:, 2:3], ["mv2"], ["mv3"])
                Zq = [("z", tt % 2, n) for n in range(4)]
                TS("dve", zb, zb, mv[:, 0:1], mv[:, 3:4], ALU.subtract, ALU.mult, Zq + ["mv", "mv3"], Zq)
                TT("pool", zb, zb, grep, ALU.mult, Zq + ["grep"], Zq)
                TT("pool", zb, zb, brep, ALU.add, Zq + ["brep"], Zq)
                DMA("pool", "o_y%d" % (tt % 2), xout_d.ap()[rsl, :], zb, Zq, ["xres%d" % tt])
            if fuse_next:
                next_xT(15)
        K.emit(st)
    return nc


def _host_tables(rel_bias):
    rel_bias = np.asarray(rel_bias, dtype=np.float32)
    kl = np.arange(128)[:, None]
    m = np.arange(S)[None, :]
    dist = m - kl
    bk = t5_bucket_np(np.clip(dist, 0, S - 1))
    biasA = np.empty((4, 128, S), np.float32)
    for h in range(4):
        biasA[h] = np.where(dist >= 0, rel_bias[bk, h], np.float32(MASK))
    biasB = np.empty((3, 8, 128, 256), np.float32)
    kj = np.arange(128)[:, None]
    qq = np.arange(256)[None, :]
    lag = qq - kj
    valid = (lag >= 0) & (lag <= 128)
    for p, (win, dil) in enumerate(PATTERNS):
        bkt = t5_bucket_np(np.clip(lag, 0, 255) * dil)
        for h in range(8):
            biasB[p, h] = np.where(valid, rel_bias[bkt, 4 + h], np.float32(MASK))
    return biasA, biasB


_NC_CACHE = {}


def kernel(x, w_in, diff_lambda, diff_head_gain, conv_dw, conv_b, conv_ln_g, conv_ln_b,
           conv_pw, w_out, ln_g, ln_b, rel_bias):
    f = lambda a: np.ascontiguousarray(np.asarray(a, dtype=np.float32))
    x = f(x)
    biasA, biasB = _host_tables(rel_bias)
    cpar = np.concatenate([f(conv_dw), f(conv_b)[:, None, :], f(conv_ln_g)[:, None, :], f(conv_ln_b)[:, None, :]],
                          axis=1)
    common = {
        "w_in": f(w_in), "w_out": f(w_out), "conv_pw": f(conv_pw),
        "dlam": f(diff_lambda).reshape(DEPTH, 256), "hgain": f(diff_head_gain),
        "cpar": np.ascontiguousarray(cpar), "ln_g": f(ln_g), "ln_b": f(ln_b),
        "biasA": biasA, "biasB": biasB, "ident": np.eye(128, dtype=np.float32),
    }
    if "nc" not in _NC_CACHE:
        _NC_CACHE["nc"] = build()
    nc = _NC_CACHE["nc"]
    in_maps = [dict(common, x=x[c]) for c in range(8)]
    res = run_bass_kernel_spmd(nc, in_maps, core_ids=list(range(8)))
    return np.stack([np.asarray(r["y"], dtype=np.float32) for r in res.results], axis=0)
```

```python
import math
from contextlib import ExitStack

import numpy as np
import concourse.bass as bass
import concourse.mybir as mybir
from concourse.bass_utils import run_bass_kernel_spmd

F32 = mybir.dt.float32
BF16 = mybir.dt.bfloat16
AF = mybir.ActivationFunctionType
ALU = mybir.AluOpType
AX = mybir.AxisListType

S = 2048
D = 2048
DIN = 7680
DEPTH = 2
MASK = -30000.0
EPS = 1e-5
ALPHA = (2 * DEPTH) ** 0.25
PATTERNS = ((128, 1), (512, 4), (2048, 16))


class _Op:
    __slots__ = ("eng", "fn", "deps", "dma_sem", "dma_val", "signals", "sig_no", "pos")

    def __init__(self, eng, fn):
        self.eng = eng
        self.fn = fn
        self.deps = []
        self.dma_sem = None
        self.dma_val = 0
        self.signals = False
        self.sig_no = 0
        self.pos = 0


class Sched:
    ENGS = ("pe", "act", "dve", "pool", "sp")

    def __init__(self, nc):
        self.nc = nc
        self.ops = []
        self.last_w = {}
        self.readers = {}
        self.dma_cnt = {}

    tags = ()

    def _add(self, eng, fn, reads, writes, arena=True):
        op = _Op(eng, fn)
        reads = list(reads)
        if arena:
            reads.extend(self.tags)
        deps = set()
        for r in reads:
            w = self.last_w.get(r)
            if w is not None:
                deps.add(w)
        for w_ in writes:
            rds = self.readers.get(w_, ())
            if rds:
                for rd in rds:
                    deps.add(rd)
            else:
                w = self.last_w.get(w_)
                if w is not None:
                    deps.add(w)
        idx = len(self.ops)
        op.deps = deps
        self.ops.append(op)
        for r in reads:
            self.readers.setdefault(r, []).append(idx)
        for w_ in writes:
            self.last_w[w_] = idx
            self.readers[w_] = []
        return op

    def op(self, eng, fn, reads=(), writes=()):
        return self._add(eng, fn, reads, writes)

    def dma(self, eng, slot, fn, reads=(), writes=()):
        op = self._add(eng, fn, reads, writes)
        self.dma_cnt[slot] = self.dma_cnt.get(slot, 0) + 1
        op.dma_sem = slot
        op.dma_val = 16 * self.dma_cnt[slot]
        return op

    def boundary(self, fn, tags):
        self._add("dve", fn, (), tuple(tags), arena=False)

    def emit(self, stack):
        nc = self.nc
        ops = self.ops
        per_eng = {e: [] for e in self.ENGS}
        for i, op in enumerate(ops):
            op.pos = len(per_eng[op.eng])
            per_eng[op.eng].append(i)
        waited = {e: {} for e in self.ENGS}
        need = [None] * len(ops)
        cur = {}
        for i, op in enumerate(ops):
            chans = {}
            for d in op.deps:
                p = ops[d]
                if p.dma_sem is not None:
                    ch = ("dma", p.dma_sem)
                    v = cur[p.dma_sem]
                else:
                    if p.eng == "pe" and op.eng == "pe":
                        continue
                    ch = ("eng", p.eng)
                    v = p.pos
                if ch not in chans or chans[ch][0] < v:
                    chans[ch] = (v, d)
            lst = []
            for ch, (v, d) in chans.items():
                prev = waited[op.eng].get(ch, -1)
                if prev >= v:
                    continue
                waited[op.eng][ch] = v
                lst.append((ch, v if ch[0] == "dma" else d))
                if ch[0] == "eng":
                    ops[d].signals = True
            need[i] = lst
            op.deps = None
            if op.dma_sem is not None:
                cur[op.dma_sem] = op.dma_val
        cnt = {e: 0 for e in self.ENGS}
        for op in ops:
            if op.dma_sem is None and op.signals:
                cnt[op.eng] += 1
                op.sig_no = cnt[op.eng]
        esem = {e: stack.enter_context(nc.semaphore("c_" + e)) for e in self.ENGS}
        dsem = {s: stack.enter_context(nc.semaphore("d_%d" % k)) for k, s in enumerate(self.dma_cnt)}
        dma_owner = {}
        for op in ops:
            if op.dma_sem is not None:
                dma_owner[op.dma_sem] = op.eng
        block = stack.enter_context(nc.Block())
        regs = {"pe": block.tensor, "act": block.scalar, "dve": block.vector,
                "pool": block.gpsimd, "sp": block.sync}

        def make_body(ename):
            def body(eng):
                for i in per_eng[ename]:
                    op = ops[i]
                    for ch, d in need[i]:
                        if ch[0] == "dma":
                            eng.wait_ge(dsem[ch[1]], d)
                        else:
                            eng.wait_ge(esem[ch[1]], ops[d].sig_no)
                    inst = op.fn(eng)
                    if op.dma_sem is not None:
                        inst.then_inc(dsem[op.dma_sem], 16)
                    elif op.signals:
                        inst.then_inc(esem[ename], 1)
                for s_, c in self.dma_cnt.items():
                    if dma_owner.get(s_) == ename:
                        eng.wait_ge(dsem[s_], 16 * c)
            return body

        for e in self.ENGS:
            regs[e](make_body(e))


def t5_bucket_np(d):
    d = np.maximum(np.asarray(d, dtype=np.int64), 0)
    df = np.maximum(d, 1).astype(np.float32)
    large = 16 + (np.log(df / np.float32(16)) / np.float32(math.log(2048 / 16)) * np.float32(16)).astype(np.int32)
    large = np.minimum(large, 31)
    return np.where(d < 16, d, large).astype(np.int64)


def build(n_layers=DEPTH, taps=()):
    nc = bass.Bass("TRN2", target_bir_lowering=False)

    def din(name, shape, dt=F32):
        return nc.dram_tensor(name, shape, dt, kind="ExternalInput")

    def dscr(name, shape, dt):
        return nc.dram_tensor(name, shape, dt, kind="Internal")

    x_d = din("x", [S, D])
    win_d = din("w_in", [DEPTH, D, DIN])
    wout_d = din("w_out", [DEPTH, D, D])
    wpw_d = din("conv_pw", [DEPTH, 512, 512])
    dlam_d = din("dlam", [DEPTH, 256])
    hgain_d = din("hgain", [DEPTH, 128])
    cpar_d = din("cpar", [DEPTH, 34, 512])
    lng_d = din("ln_g", [DEPTH, D])
    lnb_d = din("ln_b", [DEPTH, D])
    biasA_d = din("biasA", [4, 128, S])
    biasB_d = din("biasB", [3, 8, 128, 256])
    ident_d = din("ident", [128, 128])
    y_d = nc.dram_tensor("y", [S, D], F32, kind="ExternalOutput")

    xres_d = y_d
    qkA_d = dscr("qkA", [4, 2, 128, S], BF16)
    vA_d = dscr("vA", [S, 512], BF16)
    gA_d = dscr("gA", [S, 512], BF16)
    qkB_d = dscr("qkB", [8, 2, 128, S], BF16)
    vB_d = dscr("vB", [S, 1024], BF16)
    gBT_d = dscr("gBT", [1024, S], BF16)
    cuT_d = dscr("cuT", [512, S], F32)
    sgT_d = dscr("sgT", [512, S], F32)
    cgT_d = dscr("cgT", [512, S], BF16)
    wob_d = dscr("wob", [D, D], BF16)
    catT_d = dscr("catT", [D, S], BF16)
    tap_d = {}
    for t in taps:
        tap_d[t] = nc.dram_tensor("tap_" + t, [D, S], BF16, kind="ExternalOutput")

    def DAP(h, offset, pairs):
        return bass.AP(tensor=h, offset=offset, ap=[[int(a), int(b)] for a, b in pairs])

    ARENA_BYTES = 206 * 1024
    arena = nc.alloc_sbuf_tensor("arena", [128, ARENA_BYTES // 2], BF16)

    class Alloc:
        def __init__(self, base, limit):
            self.off = base
            self.limit = limit

        def get(self, dt, *free):
            n = 1
            for f in free:
                n *= f
            nbytes = n * (4 if dt == F32 else 2)
            off = self.off
            self.off = (off + nbytes + 63) // 64 * 64
            assert self.off <= self.limit, ("arena overflow", self.off, self.limit)
            v = arena[:, off // 2:(off + nbytes) // 2]
            if dt == F32:
                v = v.bitcast(F32)
            if len(free) == 2:
                v = v.rearrange("p (a b) -> p a b", b=free[1])
            elif len(free) == 3:
                v = v.rearrange("p (a b c) -> p a b c", b=free[1], c=free[2])
            return v

    pers = Alloc(0, ARENA_BYTES)
    big = pers.get(BF16, 16, S)
    identf = pers.get(F32, 128)
    identb = pers.get(BF16, 128)
    onesb = pers.get(BF16, 128)
    onesf = pers.get(F32, 128)
    neglam = pers.get(F32, 1)
    gainrep = pers.get(F32, 128)
    cp = pers.get(F32, 4, 34)
    epst = pers.get(F32, 1)
    junk = pers.get(F32, 16)
    R1_BASE = pers.off
    R1_BYTES = 58 * 1024
    R2_BASE = R1_BASE + R1_BYTES

    def A1():
        return Alloc(R1_BASE, R2_BASE)

    def A2():
        return Alloc(R2_BASE, ARENA_BYTES)

    def A12():
        return Alloc(R1_BASE, ARENA_BYTES)

    psall = nc.alloc_psum_tensor("psall", [128, 4096], F32)[:, :]
    ps = [psall[:, i * 512:(i + 1) * 512] for i in range(8)]
    psh = [psall[:, s_ * 256:(s_ + 1) * 256] for s_ in range(4)]

    def bc_inner(a2, m):
        return bass.AP(tensor=a2.tensor, offset=a2.offset, ap=[list(a2.ap[0]), list(a2.ap[1]), [0, m]])

    def bc_mid(a2, r):
        return bass.AP(tensor=a2.tensor, offset=a2.offset, ap=[list(a2.ap[0]), [0, r], list(a2.ap[1])])

    with ExitStack() as st:
        K = Sched(nc)
        TALL = ("R1", "R2")

        def MM(out, lhsT, rhs, start, stop, r, w):
            K.op("pe", lambda e: e.matmul(out, lhsT=lhsT, rhs=rhs, start=start, stop=stop), r, w)

        def TR(out, in_, ident, r, w):
            K.op("pe", lambda e: e.transpose(out, in_, ident), r, w)

        def ACTF(out, in_, func, r, w, scale=1.0, bias=0.0):
            K.op("act", lambda e: e.activation(out=out, in_=in_, func=func, bias=bias, scale=scale), r, w)

        def TT(eng, out, in0, in1, op, r, w):
            K.op(eng, lambda e: e.tensor_tensor(out=out, in0=in0, in1=in1, op=op), r, w)

        def TS(eng, out, in0, s1, s2, op0, op1, r, w):
            if s2 is None:
                K.op(eng, lambda e: e.tensor_scalar(out=out, in0=in0, scalar1=s1, scalar2=None, op0=op0), r, w)
            else:
                K.op(eng, lambda e: e.tensor_scalar(out=out, in0=in0, scalar1=s1, scalar2=s2, op0=op0, op1=op1), r, w)

        def STT(out, in0, scalar, in1, op0, op1, r, w):
            K.op("dve", lambda e: e.scalar_tensor_tensor(out=out, in0=in0, scalar=scalar, in1=in1, op0=op0, op1=op1), r, w)

        def CP(eng, out, in_, r, w):
            if eng == "act":
                K.op("act", lambda e: e.copy(out=out, in_=in_), r, w)
            else:
                K.op(eng, lambda e: e.tensor_copy(out=out, in_=in_), r, w)

        def RECIP(out, in_, r, w):
            K.op("dve", lambda e: e.reciprocal(out=out, in_=in_), r, w)

        def MEMSET(eng, ap, val, r, w):
            K.op(eng, lambda e: e.memset(ap, val), r, w)

        def DMA(eng, slot, out, in_, r, w):
            K.dma(eng, slot, lambda e: e.dma_start(out=out, in_=in_), r, w)

        def BOUNDARY(tags=TALL):
            K.boundary(lambda e: e.memset(junk[:, 0:8], 0.0), tags)

        def drain(gen, tags):
            K.tags = tags
            for _ in gen:
                pass

        K.tags = TALL
        DMA("sp", "c_id", identf, ident_d.ap(), [], ["identf"])
        CP("dve", identb, identf, ["identf"], ["identb"])
        MEMSET("dve", onesb, 1.0, [], ["onesb"])
        MEMSET("dve", onesf, 1.0, [], ["onesf"])
        MEMSET("dve", epst, EPS, [], ["epst"])

        bank_rr = [0]

        def next_bank(lo=0, hi=8):
            b = lo + bank_rr[0] % (hi - lo)
            bank_rr[0] += 1
            return b

        for layer in range(n_layers):
            lam_init = 0.8 - 0.6 * math.exp(-0.3 * layer)
            xin_d = x_d if layer == 0 else xres_d
            xout_d = y_d if layer == n_layers - 1 else xres_d

            BOUNDARY()
            K.tags = TALL
            al = A2()
            dl = al.get(F32, 256)
            tmpa = al.get(F32, 64)
            tmpb = al.get(F32, 64)
            s12 = al.get(F32, 2)
            e12 = al.get(F32, 2)
            cps = al.get(F32, 512)
            xin = [al.get(F32, D) for _ in range(4)]
            DMA("sp", "p_dl", dl, DAP(dlam_d, layer * 256, [(0, 128), (1, 256)]), [], ["dl"])
            TT("dve", tmpa, dl[:, 0:64], dl[:, 64:128], ALU.mult, ["dl"], ["tmpa"])
            TT("dve", tmpb, dl[:, 128:192], dl[:, 192:256], ALU.mult, ["dl"], ["tmpb"])
            K.op("dve", lambda e, o=s12[:, 0:1], i=tmpa: e.reduce_sum(out=o, in_=i, axis=AX.X), ["tmpa"], ["s1"])
            K.op("dve", lambda e, o=s12[:, 1:2], i=tmpb: e.reduce_sum(out=o, in_=i, axis=AX.X), ["tmpb"], ["s2"])
            ACTF(e12, s12, AF.Exp, ["s1", "s2"], ["e12"])
            TT("dve", tmpa[:, 0:1], e12[:, 0:1], e12[:, 1:2], ALU.subtract, ["e12"], ["lamv"])
            TS("dve", neglam, tmpa[:, 0:1], -1.0, -lam_init, ALU.mult, ALU.add, ["lamv"], ["neglam"])
            DMA("sp", "p_hg", gainrep, DAP(hgain_d, layer * 128, [(0, 128), (1, 128)]), [], ["gainraw"])
            TS("dve", gainrep, gainrep, 1.0 - lam_init, None, ALU.mult, None, ["gainraw"], ["gainrep"])
            DMA("sp", "p_cp", cps[0:34, :], cpar_d.ap()[layer], [], ["cps"])
            for cc in range(4):
                TR(ps[7][:, cc * 64:cc * 64 + 34], cps[0:34, cc * 128:(cc + 1) * 128], identf[0:34, 0:34],
                   ["cps", "identf"], [("ps", 7)])
            for cc in range(4):
                CP("dve", cp[:, cc, :], ps[7][:, cc * 64:cc * 64 + 34], [("ps", 7)], ["cp"])

            ev = 0
            for tt in (range(16) if layer == 0 else ()):
                xb = xin[tt % 4]
                DMA("sp" if tt % 2 == 0 else "act", "x0_%d" % (tt % 4), xb, xin_d.ap()[tt * 128:(tt + 1) * 128, :],
                    ["xres%d" % tt], [("xin", tt % 4)])
                for k4 in range(4):
                    bk = next_bank(0, 4)
                    for q in range(4):
                        kc = k4 * 4 + q
                        TR(ps[bk][:, q * 128:(q + 1) * 128], xb[:, kc * 128:(kc + 1) * 128], identf,
                           [("xin", tt % 4), "identf"], [("ps", bk)])
                    outv = big[:, k4 * 4:(k4 + 1) * 4, tt * 128:(tt + 1) * 128]
                    inv = ps[bk].rearrange("p (a b) -> p a b", b=128)
                    CP("act" if ev % 2 == 0 else "dve", outv, inv, [("ps", bk)],
                       [("big", k4 * 4 + q) for q in range(4)])
                    ev += 1
            for q_ in range(4):
                DMA("pool", "wobc", wob_d.ap()[q_ * 512:(q_ + 1) * 512, :],
                    wout_d.ap()[layer, q_ * 512:(q_ + 1) * 512, :], [], [("wob", q_)])
            BOUNDARY()

            al = A1()
            wst = [al.get(F32, 4, 512) for _ in range(2)]
            wb = [al.get(BF16, 16, 512) for _ in range(2)]
            stg = [al.get(F32, 512) for _ in range(4)]
            gstg = [al.get(BF16, 512) for _ in range(2)]
            stg_i = [0]
            gst_i = [0]
            wq_i = [0]

            def load_w(nblk, buf):
                for k4 in range(4):
                    q_ = wq_i[0] % 2
                    wq_i[0] += 1
                    src = DAP(win_d, layer * D * DIN + k4 * 4 * 128 * DIN + nblk * 512,
                              [(DIN, 128), (128 * DIN, 4), (1, 512)])
                    DMA("sp", "wst%d" % q_, wst[q_], src, [], [("wst", q_)])
                    for k_ in range(4):
                        kc = k4 * 4 + k_
                        CP("dve" if k_ % 2 == 0 else "pool", wb[buf][:, kc, :], wst[q_][:, k_, :],
                           [("wst", q_)], [("wb", buf, kc)])

            def store_stage(func, scale, dt, psb, bk, dst_ap, wkey, gain=False):
                si = stg_i[0] % 4
                stg_i[0] += 1
                sv = stg[si] if dt == F32 else stg[si].bitcast(BF16)[:, 0:512]
                if gain:
                    gi = gst_i[0] % 2
                    gst_i[0] += 1
                    ACTF(stg[si], psb, func, [("ps", bk)], [("st", si)], scale=scale)
                    TT("dve", gstg[gi].rearrange("p (a b) -> p a b", b=128),
                       stg[si].rearrange("p (a b) -> p a b", b=128), bc_mid(gainrep, 4), ALU.mult,
                       [("st", si), "gainrep"], [("gst", gi)])
                    DMA("act", "gst%d" % gi, dst_ap, gstg[gi], [("gst", gi)], [wkey])
                    return
                else:
                    ACTF(sv, psb, func, [("ps", bk)], [("st", si)], scale=scale)
                DMA("act", "st%d" % si, dst_ap, sv, [("st", si)], [wkey])

            BLK = {
                "aq": (0, AF.Copy, 0.125, True), "ak": (1, AF.Copy, 1.0, True),
                "av": (2, AF.Copy, 1.0, False), "ag": (3, AF.Silu, 1.0, False),
                "bq0": (4, AF.Copy, 128 ** -0.5, True), "bq1": (5, AF.Copy, 128 ** -0.5, True),
                "bk0": (6, AF.Copy, 1.0, True), "bk1": (7, AF.Copy, 1.0, True),
                "bv0": (8, AF.Copy, 1.0, False), "bv1": (9, AF.Copy, 1.0, False),
                "bg0": (10, AF.Silu, 1.0, True), "bg1": (11, AF.Silu, 1.0, True),
                "cu": (12, AF.Copy, 1.0, True), "cglu": (13, AF.Sigmoid, 1.0, True), "cg": (14, AF.Silu, 1.0, True),
            }
            ORDER = ["aq", "ak", "av", "ag", "bq0", "bk0", "bv0", "bg0", "bq1", "bk1", "bv1", "bg1", "cu", "cglu", "cg"]
            pj_state = {"i": 0}

            def gen_proj(count, banks=(6,)):
                i0 = pj_state["i"]
                if i0 == 0:
                    load_w(BLK[ORDER[0]][0], 0)
                for i_ in range(i0, i0 + count):
                    nm = ORDER[i_]
                    n, func, scale, fmaj = BLK[nm]
                    buf = i_ % 2
                    if i_ + 1 < len(ORDER):
                        load_w(BLK[ORDER[i_ + 1]][0], (i_ + 1) % 2)
                    for grp in range(16):
                        bk = banks[grp % len(banks)]
                        if fmaj:
                            fs, tb = grp // 4, grp % 4
                            for kc in range(16):
                                MM(ps[bk], wb[buf][:, kc, fs * 128:(fs + 1) * 128], big[:, kc, tb * 512:(tb + 1) * 512],
                                   kc == 0, kc == 15, [("wb", buf, kc), ("big", kc)], [("ps", bk)])
                                if kc % 4 == 3:
                                    yield
                            tsl = slice(tb * 512, (tb + 1) * 512)
                            if nm in ("aq", "ak"):
                                dst = qkA_d.ap()[fs, 0 if nm == "aq" else 1, :, tsl]
                                key = ("qkA", fs, nm, tb)
                                dt = BF16
                            elif nm[:2] in ("bq", "bk"):
                                hh = int(nm[2]) * 4 + fs
                                dst = qkB_d.ap()[hh, 0 if nm[:2] == "bq" else 1, :, tsl]
                                key = ("qkB", hh, nm[:2], tb)
                                dt = BF16
                            elif nm[:2] == "bg":
                                hh = int(nm[2]) * 4 + fs
                                dst = gBT_d.ap()[hh * 128:(hh + 1) * 128, tsl]
                                key = ("gBT", hh, tb)
                                dt = BF16
                            elif nm == "cu":
                                dst = cuT_d.ap()[fs * 128:(fs + 1) * 128, tsl]
                                key = ("cuT", fs, tb)
                                dt = F32
                            elif nm == "cglu":
                                dst = sgT_d.ap()[fs * 128:(fs + 1) * 128, tsl]
                                key = ("sgT", fs, tb)
                                dt = F32
                            else:
                                dst = cgT_d.ap()[fs * 128:(fs + 1) * 128, tsl]
                                key = ("cgT", fs, tb)
                                dt = BF16
                            store_stage(func, scale, dt, ps[bk], bk, dst, key)
                        else:
                            tt = grp
                            for kc in range(16):
                                MM(ps[bk], big[:, kc, tt * 128:(tt + 1) * 128], wb[buf][:, kc, :],
                                   kc == 0, kc == 15, [("wb", buf, kc), ("big", kc)], [("ps", bk)])
                                if kc % 4 == 3:
                                    yield
                            rsl = slice(tt * 128, (tt + 1) * 128)
                            if nm == "av":
                                dst = vA_d.ap()[rsl, :]
                                key = ("vA", tt)
                            elif nm == "ag":
                                dst = gA_d.ap()[rsl, :]
                                key = ("gA", tt)
                            else:
                                half = int(nm[2])
                                dst = vB_d.ap()[rsl, half * 512:(half + 1) * 512]
                                key = ("vB", half, tt)
                            store_stage(func, scale, BF16, ps[bk], bk, dst, key, gain=(nm == "ag"))
                pj_state["i"] = i0 + count

            def gen_A():
                al = A2()
                QT = [[al.get(BF16, S) for _ in range(2)] for _ in range(2)]
                KT = [al.get(BF16, S) for _ in range(2)]
                VA = [al.get(BF16, 16, 130) for _ in range(2)]
                RA = [al.get(F32, S) for _ in range(2)]
                GA = [al.get(BF16, 16, 128) for _ in range(2)]
                ssb = [al.get(F32, 256) for _ in range(3)]
                NPT = 6
                pT = [al.get(BF16, 256) for _ in range(6)]
                on0 = [al.get(F32, 2, 128) for _ in range(2)]
                rr = [al.get(F32, 16) for _ in range(2)]
                wa = [al.get(F32, 2, 128) for _ in range(2)]
                wsq = [al.get(F32, 2, 128) for _ in range(2)]
                wu = [al.get(F32, 2, 128) for _ in range(2)]
                wub = [al.get(BF16, 2, 128) for _ in range(2)]
                ost = [al.get(BF16, 256) for _ in range(2)]
                for b_ in range(2):
                    MEMSET("pool", VA[b_][:, :, 128:130], 1.0, [], [("VA", b_)])
                    MEMSET("pool", QT[b_][0][64:128, :], 0.0, [], [("QTz", b_, 0)])
                    MEMSET("pool", QT[b_][1][0:64, :], 0.0, [], [("QTz", b_, 1)])
                psT = ps[7].bitcast(BF16)
                pshA = [(0, ps[0]), (1, ps[1]), (6, ps[6])]
                LAG = 3

                def A_loads(h):
                    buf = h % 2
                    for c_ in range(2):
                        DMA("sp", "aQ%d%d" % (buf, c_), QT[buf][c_][c_ * 64:(c_ + 1) * 64, :],
                            qkA_d.ap()[h, 0, c_ * 64:(c_ + 1) * 64, :],
                            [("qkA", h, "aq", tb) for tb in range(4)], [("QT", buf, c_)])
                    DMA("sp", "aK%d" % buf, KT[buf], qkA_d.ap()[h, 1],
                        [("qkA", h, "ak", tb) for tb in range(4)], [("KT", buf)])
                    DMA("sp", "aV%d" % buf, VA[buf][:, :, 0:128],
                        DAP(vA_d, h * 128, [(512, 128), (128 * 512, 16), (1, 128)]),
                        [("vA", tt) for tt in range(16)], [("VA", buf)])
                    DMA("sp", "aR%d" % buf, RA[buf], biasA_d.ap()[h], [], [("RA", buf)])
                    DMA("sp", "aG%d" % buf, GA[buf],
                        DAP(gA_d, h * 128, [(512, 128), (128 * 512, 16), (1, 128)]),
                        [("gA", tt) for tt in range(16)], [("GA", buf)])

                stepsA = []
                gcount = 0
                for h in range(4):
                    for gg in range(8):
                        for c in range(2):
                            for j in range(2 * gg + 2):
                                stepsA.append((h, gg, c, j, gcount))
                            gcount += 1
                infoA = {}
                deferred = []

                def A_front(idx):
                    h, gg, c, j, gc = stepsA[idx]
                    buf = h % 2
                    a = max(0, j - 2 * gg)
                    q0 = gg * 256 + a * 128
                    N = 256 - a * 128
                    sbk, sap = pshA[idx % len(pshA)]
                    pb_ = idx % NPT
                    MM(sap[:, 0:N], KT[buf][:, j * 128:(j + 1) * 128],
                       QT[buf][c][:, q0:q0 + N], True, True,
                       [("KT", buf), ("QT", buf, c), ("QTz", buf, c)], [("ps", sbk)])
                    si3 = idx % 3
                    TT("dve", ssb[si3][:, 0:N], sap[:, 0:N], RA[buf][:, q0 - j * 128:q0 - j * 128 + N],
                       ALU.add, [("ps", sbk), ("RA", buf)], [("ssb", si3)])
                    ACTF(pT[pb_][:, 0:N], ssb[si3][:, 0:N], AF.Exp, [("ssb", si3)], [("pT", pb_)])
                    infoA[idx] = (a, pb_)

                def A_back(idx):
                    h, gg, c, j, gc = stepsA[idx]
                    buf = h % 2
                    a, pb_ = infoA[idx]
                    par = gc % 2
                    for i in range(2 * gg + a, 2 * gg + 2):
                        off = (i - 2 * gg - a) * 128
                        bk = 2 + 2 * par + (i - 2 * gg)
                        MM(ps[bk][:, 0:129], pT[pb_][:, off:off + 128], VA[buf][:, j, 0:129],
                           j == 0, j == i, [("pT", pb_), ("VA", buf)], [("ps", bk)])
                    if j == 2 * gg + 1:
                        A_post(h, gg, c, gc)

                def A_post(h, gg, c, gc):
                    buf = h % 2
                    par = gc % 2
                    b0 = 2 + 2 * par
                    acc3 = psall[:, b0 * 512:(b0 + 2) * 512].rearrange("p (a b) -> p a b", b=512)
                    accO = acc3[:, :, 0:128]
                    accl = acc3[:, :, 128:129]
                    pk = [("ps", b0), ("ps", b0 + 1)]
                    w_ = (gc // 2) % 2
                    R_ = rr[w_]
                    rk = ("rr", w_)
                    if c == 0:
                        RECIP(R_[:, 0:2].rearrange("p (a b) -> p a b", b=1), accl, pk, [(rk, 0)])
                        for a_ in range(2):
                            ACTF(on0[w_][:, a_, :], ps[b0 + a_][:, 0:128], AF.Copy, [("ps", b0 + a_), (rk, 0)],
                                 [("on0", w_, a_)], scale=R_[:, a_:a_ + 1])
                        return

                    def st1():
                        RECIP(R_[:, 2:4].rearrange("p (a b) -> p a b", b=1), accl, pk, [(rk, 1)])
                        TS("dve", R_[:, 4:6], R_[:, 2:4], neglam, None, ALU.mult, None, [(rk, 1), "neglam"], [(rk, 2)])
                        TT("dve", wa[w_], accO, bc_inner(R_[:, 4:6], 128), ALU.mult, pk + [(rk, 2)], [("wa", w_)])
                        TT("dve", wa[w_], wa[w_], on0[w_], ALU.add,
                           [("wa", w_), ("on0", w_, 0), ("on0", w_, 1)], [("wa", w_)])

                    def st2():
                        ACTF(wsq[w_], wa[w_], AF.Square, [("wa", w_)], [("wsq", w_)])
                        K.op("dve", lambda e, o=R_[:, 6:8], i_=wsq[w_]: e.reduce_sum(out=o, in_=i_, axis=AX.X),
                             [("wsq", w_)], [(rk, 3)])

                    def st3():
                        ACTF(R_[:, 8:10], R_[:, 6:8], AF.Ln, [(rk, 3), "epst"], [(rk, 4)], scale=1.0 / 128.0, bias=epst)
                        ACTF(R_[:, 10:12], R_[:, 8:10], AF.Exp, [(rk, 4)], [(rk, 5)], scale=-0.5)
                        for a_ in range(2):
                            ACTF(wu[w_][:, a_, :], wa[w_][:, a_, :], AF.Copy, [("wa", w_), (rk, 5)], [("wu", w_, a_)],
                                 scale=R_[:, 10 + a_:11 + a_])

                    def st4():
                        TT("dve", wub[w_], wu[w_], GA[buf][:, 2 * gg:2 * gg + 2, :], ALU.mult,
                           [("wu", w_, 0), ("wu", w_, 1), ("GA", buf)], [("wub", w_)])

                    def st5():
                        for a_ in range(2):
                            TR(psT[:, w_ * 256 + a_ * 128:w_ * 256 + (a_ + 1) * 128], wub[w_][:, a_, :], identb,
                               [("wub", w_), "identb"], [("ps", 7)])
                        CP("act", ost[w_], psT[:, w_ * 256:(w_ + 1) * 256], [("ps", 7)], [("ost", w_)])
                        DMA("pool", "aO%d" % w_, catT_d.ap()[h * 128:(h + 1) * 128, gg * 256:(gg + 1) * 256], ost[w_],
                            [("ost", w_)], [("catT", h, gg)])

                    pid = gc // 2
                    while deferred and deferred[0][0] <= pid - 2:
                        f_ = deferred.pop(0)[1]
                        if f_ is not None:
                            f_()
                    st1()
                    deferred.extend([(pid, st2), (pid, st3), (pid, st4)] + [(pid, None)] * 5 + [(pid, st5)])

                A_loads(0)
                A_loads(1)
                nA = len(stepsA)
                for idx in range(nA):
                    A_front(idx)
                    if idx >= LAG:
                        A_back(idx - LAG)
                        hp = stepsA[idx - LAG][0]
                        if stepsA[idx - LAG + 1][0] != hp and hp + 2 < 4:
                            while deferred:
                                f_ = deferred.pop(0)[1]
                                if f_ is not None:
                                    f_()
                            A_loads(hp + 2)
                    if deferred:
                        f_ = deferred.pop(0)[1]
                        if f_ is not None:
                            f_()
                    yield
                for k_ in range(LAG, 0, -1):
                    A_back(nA - k_)
                while deferred:
                    f_ = deferred.pop(0)[1]
                    if f_ is not None:
                        f_()

            def gen_B():
                al = A12()
                QN = [al.get(BF16, S) for _ in range(2)]
                KN = [al.get(BF16, S) for _ in range(2)]
                VP = [al.get(BF16, 3, 16, 128) for _ in range(2)]
                B2 = [al.get(F32, 3, 256) for _ in range(2)]
                GT = [al.get(BF16, S) for _ in range(2)]
                PT = [al.get(BF16, 48, 256) for _ in range(2)]
                QP = [[al.get(BF16, S) for _ in range(2)] for _ in range(2)]
                ssB = [al.get(F32, 256) for _ in range(3)]
                rlb = [al.get(F32, 512) for _ in range(2)]
                o32 = [al.get(F32, 512) for _ in range(2)]
                bst = [al.get(BF16, S) for _ in range(2)]
                SBK = (0, 1, 6)

                def B_loads(h):
                    buf = h % 2
                    DMA("sp", "bQ%d" % buf, QN[buf], qkB_d.ap()[h, 0],
                        [("qkB", h, "bq", tb) for tb in range(4)], [("QN", buf)])
                    DMA("sp", "bK%d" % buf, KN[buf], qkB_d.ap()[h, 1],
                        [("qkB", h, "bk", tb) for tb in range(4)], [("KN", buf)])
                    for p, (win, dil) in enumerate(PATTERNS):
                        Lp = S // dil
                        nb = Lp // 128
                        for r in range(dil):
                            src = DAP(vB_d, r * 1024 + h * 128, [(dil * 1024, 128), (128 * dil * 1024, nb), (1, 128)])
                            last = (p == 2 and r == dil - 1)
                            DMA("sp", "bV%d" % buf, VP[buf][:, p, r * nb:(r + 1) * nb, :], src,
                                [("vB", h // 4, tt) for tt in range(16)],
                                [("VP", buf, p, r)] + ([("VPall", buf)] if last else []))
                    DMA("sp", "bB%d" % buf, B2[buf], DAP(biasB_d, h * 128 * 256, [(256, 128), (8 * 128 * 256, 3), (1, 256)]),
                        [], [("B2", buf)])
                    DMA("sp", "bG%d" % buf, GT[buf], gBT_d.ap()[h * 128:(h + 1) * 128, :],
                        [("gBT", h, tb) for tb in range(4)], [("GT", buf)])
                    for pi, dil in ((0, 4), (1, 16)):
                        CP("pool", QP[buf][pi].rearrange("p (r l) -> p r l", r=dil),
                           QN[buf].rearrange("p (l r) -> p r l", r=dil), [("QN", buf)], [("QP", buf, pi)])

                def tid_p0(kb):
                    return 16 + (kb // 4) * 8 + kb % 4

                def tid_p1(r, b):
                    return 16 + b * 8 + 4 + r

                fronts = []
                group_last = {}
                for h in range(8):
                    for r16 in range(16):
                        fronts.append((h, r16, 2, r16, 16, 128))
                    for TB in range(4):
                        for kb in range(4 * TB, 4 * TB + 4):
                            fronts.append((h, tid_p0(kb), 0, kb * 128, 1, 256 if kb < 15 else 128))
                        for r in range(4):
                            fronts.append((h, tid_p1(r, TB), 1, TB * 512 + r, 4, 256 if TB < 3 else 128))
                        group_last[(h, TB)] = len(fronts) - 1
                fctr = [0]

                def B_front(fi):
                    h, tid, p, t0, dil, nq = fronts[fi]
                    buf = h % 2
                    sbk = SBK[fi % 3]
                    si3 = fi % 3
                    kap = KN[buf][:, t0:t0 + 127 * dil + 1:dil]
                    if p == 0:
                        qap = QN[buf][:, t0:t0 + nq]
                        qk = ("QN", buf)
                    else:
                        r_, l0 = t0 % dil, t0 // dil
                        qap = QP[buf][p - 1][:, r_ * (S // dil) + l0:r_ * (S // dil) + l0 + nq]
                        qk = ("QP", buf, p - 1)
                    MM(ps[sbk][:, 0:nq], kap, qap, True, True, [qk, ("KN", buf)], [("ps", sbk)])
                    TT("dve", ssB[si3][:, 0:nq], ps[sbk][:, 0:nq], B2[buf][:, p, 0:nq], ALU.add,
                       [("ps", sbk), ("B2", buf)], [("ssB", si3)])
                    ACTF(PT[buf][:, tid, 0:nq], ssB[si3][:, 0:nq], AF.Exp, [("ssB", si3)], [("PT", buf, tid)])

                blk_ctr = [0]

                def B_block(h, TB):
                    buf = h % 2
                    par = blk_ctr[0] % 2
                    blk_ctr[0] += 1
                    bO, bL = 2 + par, 4 + par
                    contrib = []
                    for qb in range(4 * TB, 4 * TB + 4):
                        oc = ((qb - 4 * TB) * 128, 1, 128)
                        if qb - 1 >= 0:
                            contrib.append((tid_p0(qb - 1), 128, 128, oc, VP[buf][:, 0, qb - 1, :], ("VP", buf, 0, 0)))
                        contrib.append((tid_p0(qb), 0, 128, oc, VP[buf][:, 0, qb, :], ("VP", buf, 0, 0)))
                    for r in range(4):
                        oc = (r, 4, 128)
                        if TB - 1 >= 0:
                            contrib.append((tid_p1(r, TB - 1), 128, 128, oc, VP[buf][:, 1, r * 4 + TB - 1, :], ("VP", buf, 1, r)))
                        contrib.append((tid_p1(r, TB), 0, 128, oc, VP[buf][:, 1, r * 4 + TB, :], ("VP", buf, 1, r)))
                    for r16 in range(16):
                        contrib.append((r16, 32 * TB, 32, (r16, 16, 32), VP[buf][:, 2, r16, :], ("VP", buf, 2, r16)))
                    n = len(contrib)

                    def mk(k_, tid, c0, ncol, o0, ostep, ocnt, vt, vkey):
                        def f():
                            rhs = PT[buf][:, tid, c0:c0 + ncol]
                            osl = slice(o0, o0 + (ocnt - 1) * ostep + 1, ostep)
                            MM(ps[bO][:, osl], vt, rhs, k_ == 0, k_ == n - 1,
                               [vkey, ("VPall", buf), ("PT", buf, tid)], [("ps", bO)])
                            MM(ps[bL][:, osl], onesb, rhs, k_ == 0, k_ == n - 1,
                               ["onesb", ("PT", buf, tid)], [("ps", bL)])
                        return f

                    for k_, (tid, c0, ncol, (o0, ostep, ocnt), vt, vkey) in enumerate(contrib):
                        pvq.append(mk(k_, tid, c0, ncol, o0, ostep, ocnt, vt, vkey))

                    def fin():
                        w_ = par
                        tsl = slice(TB * 512, (TB + 1) * 512)
                        ACTF(rlb[w_], ps[bL], AF.Ln, [("ps", bL)], [("rlb", w_)])
                        ACTF(rlb[w_], rlb[w_], AF.Exp, [("rlb", w_)], [("rlb", w_)], scale=-1.0)
                        TT("dve", o32[w_], ps[bO], rlb[w_], ALU.mult, [("ps", bO), ("rlb", w_)], [("o32", w_)])
                        TT("pool", bst[buf][:, tsl], o32[w_], GT[buf][:, tsl], ALU.mult,
                           [("o32", w_), ("GT", buf)], [("bst", buf, TB)])
                        if TB == 3:
                            DMA("pool", "bO%d" % buf, catT_d.ap()[(4 + h) * 128:(5 + h) * 128, :], bst[buf],
                                [("bst", buf, t_) for t_ in range(4)], [("catT", 4 + h)])
                            if h + 2 < 8:
                                B_loads(h + 2)
                    pvq.append(fin)

                LAGF = 4
                pvq = []
                pending = []
                for h in range(8):
                    for TB in range(4):
                        pending.append((group_last[(h, TB)], h, TB))
                B_loads(0)
                B_loads(1)
                for fi in range(len(fronts)):
                    B_front(fi)
                    while pending and pending[0][0] + LAGF <= fi:
                        _, h_, TB_ = pending.pop(0)
                        B_block(h_, TB_)
                    for _k in range(3):
                        if pvq:
                            pvq.pop(0)()
                    yield
                while pending:
                    _, h_, TB_ = pending.pop(0)
                    B_block(h_, TB_)
                while pvq:
                    pvq.pop(0)()

            drain(gen_proj(15, (4, 5, 6, 7)), ("R1",))
            BOUNDARY(("R2",))
            drain(gen_A(), ("R2",))
            BOUNDARY()
            drain(gen_B(), ("R2",))
            BOUNDARY()
            K.tags = TALL

            al = A12()
            wpst = al.get(F32, 4, 512)
            wpw = al.get(BF16, 4, 512)
            Dg = al.get(BF16, 4, 31, 128)
            cu = al.get(F32, S)
            sg = al.get(F32, S)
            hpad = al.get(BF16, 4, S + 32)
            y1 = al.get(F32, 4, S)
            ysq = wpst
            mean = al.get(F32, 512)
            msq = al.get(F32, 512)
            rstd = al.get(F32, 512)
            hn = [al.get(F32, 512) for _ in range(2)]
            hs = al.get(BF16, 4, 512)
            cgt = [al.get(BF16, 512) for _ in range(2)]
            DMA("sp", "c_w", wpst, DAP(wpw_d, layer * 512 * 512, [(512, 128), (128 * 512, 4), (1, 512)]), [], ["wpst"])
            CP("pool", wpw, wpst, ["wpst"], ["wpw"])
            for cc in range(4):
                for j in range(31):
                    if j % 2 == 0:
                        TS("dve", Dg[:, cc, j, :], identb, cp[:, cc, j:j + 1], None, ALU.mult, None,
                           ["identb", "cp"], [("Dg", cc, j)])
                    else:
                        ACTF(Dg[:, cc, j, :], identb, AF.Copy, ["identb", "cp"], [("Dg", cc, j)],
                             scale=cp[:, cc, j:j + 1])
            for cc in range(4):
                DMA("sp", "c_cu", cu, cuT_d.ap()[cc * 128:(cc + 1) * 128, :], [("cuT", cc, tb) for tb in range(4)], ["cu"])
                DMA("act", "c_sg", sg, sgT_d.ap()[cc * 128:(cc + 1) * 128, :], [("sgT", cc, tb) for tb in range(4)], ["sg"])
                MEMSET("pool", hpad[:, cc, 0:30], 0.0, [], [("hpad0", cc)])
                TT("dve", hpad[:, cc, 30:30 + S], cu, sg, ALU.mult, ["cu", "sg"], [("hpad", cc)])
            for hh in range(12):
                DMA("sp", "catl", big[:, hh, :], catT_d.ap()[hh * 128:(hh + 1) * 128, :], ([("catT", hh, g_) for g_ in range(8)] if hh < 4 else [("catT", hh)]), [("big", hh)])
            for cc in range(4):
                for tb in range(4):
                    bk = next_bank(0, 4)
                    for j in range(31):
                        MM(ps[bk], Dg[:, cc, j, :], hpad[:, cc, tb * 512 + j:tb * 512 + j + 512], j == 0, j == 30,
                           [("Dg", cc, j), ("hpad", cc), ("hpad0", cc)], [("ps", bk)])
                    ACTF(y1[:, cc, tb * 512:(tb + 1) * 512], ps[bk], AF.Identity, [("ps", bk), "cp"], [("y1", cc, tb)],
                         bias=cp[:, cc, 31:32])
            for tb in range(4):
                tsl = slice(tb * 512, (tb + 1) * 512)
                for cc in range(4):
                    ACTF(ysq[:, cc, :], y1[:, cc, tsl], AF.Square, [("y1", cc, tb)], [("ysq", cc), "wpst"])
                for cc in range(4):
                    MM(ps[4], onesf, y1[:, cc, tsl], cc == 0, cc == 3, ["onesf", ("y1", cc, tb)], [("ps", 4)])
                for cc in range(4):
                    MM(ps[5], onesf, ysq[:, cc, :], cc == 0, cc == 3, ["onesf", ("ysq", cc)], [("ps", 5)])
                ACTF(mean, ps[4], AF.Copy, [("ps", 4)], ["mean"], scale=1.0 / 512.0)
                TT("dve", msq, mean, mean, ALU.mult, ["mean"], ["msq"])
                STT(rstd, ps[5], 1.0 / 512.0, msq, ALU.mult, ALU.subtract, [("ps", 5), "msq"], ["rstd"])
                ACTF(rstd, rstd, AF.Ln, ["rstd", "epst"], ["rstd"], bias=epst)
                ACTF(rstd, rstd, AF.Exp, ["rstd"], ["rstd"], scale=-0.5)
                for cc in range(4):
                    hb = hn[cc % 2]
                    TT("dve", hb, y1[:, cc, tsl], mean, ALU.subtract, [("y1", cc, tb), "mean"], [("hn", cc % 2)])
                    TT("dve", hb, hb, rstd, ALU.mult, [("hn", cc % 2), "rstd"], [("hn", cc % 2)])
                    ACTF(hs[:, cc, :], hb, AF.Silu, [("hn", cc % 2), "cp"], [("hs", cc)],
                         scale=cp[:, cc, 32:33], bias=cp[:, cc, 33:34])
                for co in range(4):
                    bk = next_bank(0, 4)
                    gb = cgt[co % 2]
                    DMA("sp", "c_g%d" % (co % 2), gb, cgT_d.ap()[co * 128:(co + 1) * 128, tsl],
                        [("cgT", co, tb)], [("cgt", co % 2)])
                    for cc in range(4):
                        MM(ps[bk], wpw[:, cc, co * 128:(co + 1) * 128], hs[:, cc, :], cc == 0, cc == 3,
                           ["wpw", ("hs", cc)], [("ps", bk)])
                    TT("dve", big[:, 12 + co, tsl], ps[bk], gb, ALU.mult, [("ps", bk), ("cgt", co % 2)],
                       [("big", 12 + co)])

            BOUNDARY()
            if ("cat%d" % layer) in tap_d:
                DMA("sp", "tapcat", DAP(tap_d["cat%d" % layer], 0, [(S, 128), (128 * S, 16), (1, S)]), big,
                    [("big", kc) for kc in range(16)], ["tapcat"])
                BOUNDARY()
            al = A12()
            wo = al.get(BF16, 16, D)
            zt = [al.get(F32, D) for _ in range(2)]
            xrs = [al.get(F32, D) for _ in range(2)]
            grep = al.get(F32, D)
            brep = al.get(F32, D)
            stats = al.get(F32, 24)
            mv = al.get(F32, 4)
            DMA("act", "o_g", grep, DAP(lng_d, layer * D, [(0, 128), (1, D)]), [], ["grep"])
            DMA("act", "o_b", brep, DAP(lnb_d, layer * D, [(0, 128), (1, D)]), [], ["brep"])
            for n in range(4):
                DMA("sp", "wo4_%d" % n, wo[:, :, n * 512:(n + 1) * 512],
                    DAP(wob_d, n * 512, [(D, 128), (128 * D, 16), (1, 512)]),
                    [("wob", q_) for q_ in range(4)], [("wo", kc, n) for kc in range(16)])

            def load_x(tt):
                DMA("sp", "o_x%d" % (tt % 2), xrs[tt % 2], xin_d.ap()[tt * 128:(tt + 1) * 128, :],
                    ["xres%d" % tt], [("xr", tt % 2)])

            def next_xT(tt_):
                zb_ = zt[tt_ % 2]
                Zq_ = [("z", tt_ % 2, n_) for n_ in range(4)]
                for k4 in range(4):
                    bk_ = next_bank(0, 8)
                    for q in range(4):
                        kc_ = k4 * 4 + q
                        TR(ps[bk_][:, q * 128:(q + 1) * 128], zb_[:, kc_ * 128:(kc_ + 1) * 128], identf,
                           Zq_ + ["identf"], [("ps", bk_)])
                    CP("act", big[:, k4 * 4:(k4 + 1) * 4, tt_ * 128:(tt_ + 1) * 128],
                       ps[bk_].rearrange("p (a b) -> p a b", b=128), [("ps", bk_)], [("bigx", k4, tt_)])

            fuse_next = layer + 1 < n_layers
            load_x(0)
            for tt in range(16):
                zb = zt[tt % 2]
                xr = xrs[tt % 2]
                rsl = slice(tt * 128, (tt + 1) * 128)
                if tt + 1 < 16:
                    load_x(tt + 1)
                for n in range(4):
                    bk = next_bank(0, 8)
                    for kc in range(16):
                        MM(ps[bk], big[:, kc, rsl], wo[:, kc, n * 512:(n + 1) * 512], kc == 0, kc == 15,
                           [("big", kc), ("wo", kc, n)], [("ps", bk)])
                    STT(zb[:, n * 512:(n + 1) * 512], xr[:, n * 512:(n + 1) * 512], ALPHA, ps[bk], ALU.mult, ALU.add,
                        [("xr", tt % 2), ("ps", bk)], [("z", tt % 2, n)])
                if fuse_next and tt >= 1:
                    next_xT(tt - 1)
                for n in range(4):
                    K.op("dve", lambda e, o=stats[:, n * 6:(n + 1) * 6], i_=zb[:, n * 512:(n + 1) * 512]: e.bn_stats(out=o, in_=i_),
                         [("z", tt % 2, n)], [("stats", n)])
                K.op("dve", lambda e, o=mv[:, 0:2], i_=stats: e.bn_aggr(out=o, in_=i_),
                     [("stats", n) for n in range(4)], ["mv"])
                ACTF(mv[:, 2:3], mv[:, 1:2], AF.Sqrt, ["mv", "epst"], ["mv2"], bias=epst)
                RECIP(mv[:, 3:4], mv[:, 2:3], ["mv2"], ["mv3"])
                Zq = [("z", tt % 2, n) for n in range(4)]
                TS("dve", zb, zb, mv[:, 0:1], mv[:, 3:4], ALU.subtract, ALU.mult, Zq + ["mv", "mv3"], Zq)
                TT("pool", zb, zb, grep, ALU.mult, Zq + ["grep"], Zq)
                TT("pool", zb, zb, brep, ALU.add, Zq + ["brep"], Zq)
                DMA("pool", "o_y%d" % (tt % 2), xout_d.ap()[rsl, :], zb, Zq, ["xres%d" % tt])
            if fuse_next:
                next_xT(15)
        K.emit(st)
    return nc


def _host_tables(rel_bias):
    rel_bias = np.asarray(rel_bias, dtype=np.float32)
    kl = np.arange(128)[:, None]
    m = np.arange(S)[None, :]
    dist = m - kl
    bk = t5_bucket_np(np.clip(dist, 0, S - 1))
    biasA = np.empty((4, 128, S), np.float32)
    for h in range(4):
        biasA[h] = np.where(dist >= 0, rel_bias[bk, h], np.float32(MASK))
    biasB = np.empty((3, 8, 128, 256), np.float32)
    kj = np.arange(128)[:, None]
    qq = np.arange(256)[None, :]
    lag = qq - kj
    valid = (lag >= 0) & (lag <= 128)
    for p, (win, dil) in enumerate(PATTERNS):
        bkt = t5_bucket_np(np.clip(lag, 0, 255) * dil)
        for h in range(8):
            biasB[p, h] = np.where(valid, rel_bias[bkt, 4 + h], np.float32(MASK))
    return biasA, biasB


_NC_CACHE = {}


def kernel(x, w_in, diff_lambda, diff_head_gain, conv_dw, conv_b, conv_ln_g, conv_ln_b,
           conv_pw, w_out, ln_g, ln_b, rel_bias):
    f = lambda a: np.ascontiguousarray(np.asarray(a, dtype=np.float32))
    x = f(x)
    biasA, biasB = _host_tables(rel_bias)
    cpar = np.concatenate([f(conv_dw), f(conv_b)[:, None, :], f(conv_ln_g)[:, None, :], f(conv_ln_b)[:, None, :]],
                          axis=1)
    common = {
        "w_in": f(w_in), "w_out": f(w_out), "conv_pw": f(conv_pw),
        "dlam": f(diff_lambda).reshape(DEPTH, 256), "hgain": f(diff_head_gain),
        "cpar": np.ascontiguousarray(cpar), "ln_g": f(ln_g), "ln_b": f(ln_b),
        "biasA": biasA, "biasB": biasB, "ident": np.eye(128, dtype=np.float32),
    }
    if "nc" not in _NC_CACHE:
        _NC_CACHE["nc"] = build()
    nc = _NC_CACHE["nc"]
    in_maps = [dict(common, x=x[c]) for c in range(8)]
    res = run_bass_kernel_spmd(nc, in_maps, core_ids=list(range(8)))
    return np.stack([np.asarray(r["y"], dtype=np.float32) for r in res.results], axis=0)
```

```python
import math
from contextlib import ExitStack

import numpy as np
import concourse.bass as bass
import concourse.mybir as mybir
from concourse.bass_utils import run_bass_kernel_spmd

F32 = mybir.dt.float32
BF16 = mybir.dt.bfloat16
AF = mybir.ActivationFunctionType
ALU = mybir.AluOpType
AX = mybir.AxisListType

S = 2048
D = 2048
DIN = 7680
DEPTH = 2
MASK = -30000.0
EPS = 1e-5
ALPHA = (2 * DEPTH) ** 0.25
PATTERNS = ((128, 1), (512, 4), (2048, 16))


class _Op:
    __slots__ = ("eng", "fn", "deps", "dma_sem", "dma_val", "signals", "sig_no", "pos")

    def __init__(self, eng, fn):
        self.eng = eng
        self.fn = fn
        self.deps = []
        self.dma_sem = None
        self.dma_val = 0
        self.signals = False
        self.sig_no = 0
        self.pos = 0


class Sched:
    ENGS = ("pe", "act", "dve", "pool", "sp")

    def __init__(self, nc):
        self.nc = nc
        self.ops = []
        self.last_w = {}
        self.readers = {}
        self.dma_cnt = {}

    tags = ()

    def _add(self, eng, fn, reads, writes, arena=True):
        op = _Op(eng, fn)
        reads = list(reads)
        if arena:
            reads.extend(self.tags)
        deps = set()
        for r in reads:
            w = self.last_w.get(r)
            if w is not None:
                deps.add(w)
        for w_ in writes:
            rds = self.readers.get(w_, ())
            if rds:
                for rd in rds:
                    deps.add(rd)
            else:
                w = self.last_w.get(w_)
                if w is not None:
                    deps.add(w)
        idx = len(self.ops)
        op.deps = deps
        self.ops.append(op)
        for r in reads:
            self.readers.setdefault(r, []).append(idx)
        for w_ in writes:
            self.last_w[w_] = idx
            self.readers[w_] = []
        return op

    def op(self, eng, fn, reads=(), writes=()):
        return self._add(eng, fn, reads, writes)

    def dma(self, eng, slot, fn, reads=(), writes=()):
        op = self._add(eng, fn, reads, writes)
        self.dma_cnt[slot] = self.dma_cnt.get(slot, 0) + 1
        op.dma_sem = slot
        op.dma_val = 16 * self.dma_cnt[slot]
        return op

    def boundary(self, fn, tags):
        self._add("dve", fn, (), tuple(tags), arena=False)

    def emit(self, stack):
        nc = self.nc
        ops = self.ops
        per_eng = {e: [] for e in self.ENGS}
        for i, op in enumerate(ops):
            op.pos = len(per_eng[op.eng])
            per_eng[op.eng].append(i)
        waited = {e: {} for e in self.ENGS}
        need = [None] * len(ops)
        cur = {}
        for i, op in enumerate(ops):
            chans = {}
            for d in op.deps:
                p = ops[d]
                if p.dma_sem is not None:
                    ch = ("dma", p.dma_sem)
                    v = cur[p.dma_sem]
                else:
                    if p.eng == "pe" and op.eng == "pe":
                        continue
                    ch = ("eng", p.eng)
                    v = p.pos
                if ch not in chans or chans[ch][0] < v:
                    chans[ch] = (v, d)
            lst = []
            for ch, (v, d) in chans.items():
                prev = waited[op.eng].get(ch, -1)
                if prev >= v:
                    continue
                waited[op.eng][ch] = v
                lst.append((ch, v if ch[0] == "dma" else d))
                if ch[0] == "eng":
                    ops[d].signals = True
            need[i] = lst
            op.deps = None
            if op.dma_sem is not None:
                cur[op.dma_sem] = op.dma_val
        cnt = {e: 0 for e in self.ENGS}
        for op in ops:
            if op.dma_sem is None and op.signals:
                cnt[op.eng] += 1
                op.sig_no = cnt[op.eng]
        esem = {e: stack.enter_context(nc.semaphore("c_" + e)) for e in self.ENGS}
        dsem = {s: stack.enter_context(nc.semaphore("d_%d" % k)) for k, s in enumerate(self.dma_cnt)}
        dma_owner = {}
        for op in ops:
            if op.dma_sem is not None:
                dma_owner[op.dma_sem] = op.eng
        block = stack.enter_context(nc.Block())
        regs = {"pe": block.tensor, "act": block.scalar, "dve": block.vector,
                "pool": block.gpsimd, "sp": block.sync}

        def make_body(ename):
            def body(eng):
                for i in per_eng[ename]:
                    op = ops[i]
                    for ch, d in need[i]:
                        if ch[0] == "dma":
                            eng.wait_ge(dsem[ch[1]], d)
                        else:
                            eng.wait_ge(esem[ch[1]], ops[d].sig_no)
                    inst = op.fn(eng)
                    if op.dma_sem is not None:
                        inst.then_inc(dsem[op.dma_sem], 16)
                    elif op.signals:
                        inst.then_inc(esem[ename], 1)
                for s_, c in self.dma_cnt.items():
                    if dma_owner.get(s_) == ename:
                        eng.wait_ge(dsem[s_], 16 * c)
            return body

        for e in self.ENGS:
            regs[e](make_body(e))


def t5_bucket_np(d):
    d = np.maximum(np.asarray(d, dtype=np.int64), 0)
    df = np.maximum(d, 1).astype(np.float32)
    large = 16 + (np.log(df / np.float32(16)) / np.float32(math.log(2048 / 16)) * np.float32(16)).astype(np.int32)
    large = np.minimum(large, 31)
    return np.where(d < 16, d, large).astype(np.int64)


def build(n_layers=DEPTH, taps=()):
    nc = bass.Bass("TRN2", target_bir_lowering=False)

    def din(name, shape, dt=F32):
        return nc.dram_tensor(name, shape, dt, kind="ExternalInput")

    def dscr(name, shape, dt):
        return nc.dram_tensor(name, shape, dt, kind="Internal")

    x_d = din("x", [S, D])
    win_d = din("w_in", [DEPTH, D, DIN])
    wout_d = din("w_out", [DEPTH, D, D])
    wpw_d = din("conv_pw", [DEPTH, 512, 512])
    dlam_d = din("dlam", [DEPTH, 256])
    hgain_d = din("hgain", [DEPTH, 128])
    cpar_d = din("cpar", [DEPTH, 34, 512])
    lng_d = din("ln_g", [DEPTH, D])
    lnb_d = din("ln_b", [DEPTH, D])
    biasA_d = din("biasA", [4, 128, S])
    biasB_d = din("biasB", [3, 8, 128, 256])
    ident_d = din("ident", [128, 128])
    y_d = nc.dram_tensor("y", [S, D], F32, kind="ExternalOutput")

    xres_d = y_d
    qkA_d = dscr("qkA", [4, 2, 128, S], BF16)
    vA_d = dscr("vA", [S, 512], BF16)
    gA_d = dscr("gA", [S, 512], BF16)
    qkB_d = dscr("qkB", [8, 2, 128, S], BF16)
    vB_d = dscr("vB", [S, 1024], BF16)
    gBT_d = dscr("gBT", [1024, S], BF16)
    cuT_d = dscr("cuT", [512, S], F32)
    sgT_d = dscr("sgT", [512, S], F32)
    cgT_d = dscr("cgT", [512, S], BF16)
    wob_d = dscr("wob", [D, D], BF16)
    catT_d = dscr("catT", [D, S], BF16)
    tap_d = {}
    for t in taps:
        tap_d[t] = nc.dram_tensor("tap_" + t, [D, S], BF16, kind="ExternalOutput")

    def DAP(h, offset, pairs):
        return bass.AP(tensor=h, offset=offset, ap=[[int(a), int(b)] for a, b in pairs])

    ARENA_BYTES = 206 * 1024
    arena = nc.alloc_sbuf_tensor("arena", [128, ARENA_BYTES // 2], BF16)

    class Alloc:
        def __init__(self, base, limit):
            self.off = base
            self.limit = limit

        def get(self, dt, *free):
            n = 1
            for f in free:
                n *= f
            nbytes = n * (4 if dt == F32 else 2)
            off = self.off
            self.off = (off + nbytes + 63) // 64 * 64
            assert self.off <= self.limit, ("arena overflow", self.off, self.limit)
            v = arena[:, off // 2:(off + nbytes) // 2]
            if dt == F32:
                v = v.bitcast(F32)
            if len(free) == 2:
                v = v.rearrange("p (a b) -> p a b", b=free[1])
            elif len(free) == 3:
                v = v.rearrange("p (a b c) -> p a b c", b=free[1], c=free[2])
            return v

    pers = Alloc(0, ARENA_BYTES)
    big = pers.get(BF16, 16, S)
    identf = pers.get(F32, 128)
    identb = pers.get(BF16, 128)
    onesb = pers.get(BF16, 128)
    onesf = pers.get(F32, 128)
    neglam = pers.get(F32, 1)
    gainrep = pers.get(F32, 128)
    cp = pers.get(F32, 4, 34)
    epst = pers.get(F32, 1)
    junk = pers.get(F32, 16)
    R1_BASE = pers.off
    R1_BYTES = 58 * 1024
    R2_BASE = R1_BASE + R1_BYTES

    def A1():
        return Alloc(R1_BASE, R2_BASE)

    def A2():
        return Alloc(R2_BASE, ARENA_BYTES)

    def A12():
        return Alloc(R1_BASE, ARENA_BYTES)

    psall = nc.alloc_psum_tensor("psall", [128, 4096], F32)[:, :]
    ps = [psall[:, i * 512:(i + 1) * 512] for i in range(8)]
    psh = [psall[:, s_ * 256:(s_ + 1) * 256] for s_ in range(4)]

    def bc_inner(a2, m):
        return bass.AP(tensor=a2.tensor, offset=a2.offset, ap=[list(a2.ap[0]), list(a2.ap[1]), [0, m]])

    def bc_mid(a2, r):
        return bass.AP(tensor=a2.tensor, offset=a2.offset, ap=[list(a2.ap[0]), [0, r], list(a2.ap[1])])

    with ExitStack() as st:
        K = Sched(nc)
        TALL = ("R1", "R2")

        def MM(out, lhsT, rhs, start, stop, r, w):
            K.op("pe", lambda e: e.matmul(out, lhsT=lhsT, rhs=rhs, start=start, stop=stop), r, w)

        def TR(out, in_, ident, r, w):
            K.op("pe", lambda e: e.transpose(out, in_, ident), r, w)

        def ACTF(out, in_, func, r, w, scale=1.0, bias=0.0):
            K.op("act", lambda e: e.activation(out=out, in_=in_, func=func, bias=bias, scale=scale), r, w)

        def TT(eng, out, in0, in1, op, r, w):
            K.op(eng, lambda e: e.tensor_tensor(out=out, in0=in0, in1=in1, op=op), r, w)

        def TS(eng, out, in0, s1, s2, op0, op1, r, w):
            if s2 is None:
                K.op(eng, lambda e: e.tensor_scalar(out=out, in0=in0, scalar1=s1, scalar2=None, op0=op0), r, w)
            else:
                K.op(eng, lambda e: e.tensor_scalar(out=out, in0=in0, scalar1=s1, scalar2=s2, op0=op0, op1=op1), r, w)

        def STT(out, in0, scalar, in1, op0, op1, r, w):
            K.op("dve", lambda e: e.scalar_tensor_tensor(out=out, in0=in0, scalar=scalar, in1=in1, op0=op0, op1=op1), r, w)

        def CP(eng, out, in_, r, w):
            if eng == "act":
                K.op("act", lambda e: e.copy(out=out, in_=in_), r, w)
            else:
                K.op(eng, lambda e: e.tensor_copy(out=out, in_=in_), r, w)

        def RECIP(out, in_, r, w):
            K.op("dve", lambda e: e.reciprocal(out=out, in_=in_), r, w)

        def MEMSET(eng, ap, val, r, w):
            K.op(eng, lambda e: e.memset(ap, val), r, w)

        def DMA(eng, slot, out, in_, r, w):
            K.dma(eng, slot, lambda e: e.dma_start(out=out, in_=in_), r, w)

        def BOUNDARY(tags=TALL):
            K.boundary(lambda e: e.memset(junk[:, 0:8], 0.0), tags)

        def drain(gen, tags):
            K.tags = tags
            for _ in gen:
                pass

        K.tags = TALL
        DMA("sp", "c_id", identf, ident_d.ap(), [], ["identf"])
        CP("dve", identb, identf, ["identf"], ["identb"])
        MEMSET("dve", onesb, 1.0, [], ["onesb"])
        MEMSET("dve", onesf, 1.0, [], ["onesf"])
        MEMSET("dve", epst, EPS, [], ["epst"])

        bank_rr = [0]

        def next_bank(lo=0, hi=8):
            b = lo + bank_rr[0] % (hi - lo)
            bank_rr[0] += 1
            return b

        for layer in range(n_layers):
            lam_init = 0.8 - 0.6 * math.exp(-0.3 * layer)
            xin_d = x_d if layer == 0 else xres_d
            xout_d = y_d if layer == n_layers - 1 else xres_d

            BOUNDARY()
            K.tags = TALL
            al = A2()
            dl = al.get(F32, 256)
            tmpa = al.get(F32, 64)
            tmpb = al.get(F32, 64)
            s12 = al.get(F32, 2)
            e12 = al.get(F32, 2)
            cps = al.get(F32, 512)
            xin = [al.get(F32, D) for _ in range(4)]
            DMA("sp", "p_dl", dl, DAP(dlam_d, layer * 256, [(0, 128), (1, 256)]), [], ["dl"])
            TT("dve", tmpa, dl[:, 0:64], dl[:, 64:128], ALU.mult, ["dl"], ["tmpa"])
            TT("dve", tmpb, dl[:, 128:192], dl[:, 192:256], ALU.mult, ["dl"], ["tmpb"])
            K.op("dve", lambda e, o=s12[:, 0:1], i=tmpa: e.reduce_sum(out=o, in_=i, axis=AX.X), ["tmpa"], ["s1"])
            K.op("dve", lambda e, o=s12[:, 1:2], i=tmpb: e.reduce_sum(out=o, in_=i, axis=AX.X), ["tmpb"], ["s2"])
            ACTF(e12, s12, AF.Exp, ["s1", "s2"], ["e12"])
            TT("dve", tmpa[:, 0:1], e12[:, 0:1], e12[:, 1:2], ALU.subtract, ["e12"], ["lamv"])
            TS("dve", neglam, tmpa[:, 0:1], -1.0, -lam_init, ALU.mult, ALU.add, ["lamv"], ["neglam"])
            DMA("sp", "p_hg", gainrep, DAP(hgain_d, layer * 128, [(0, 128), (1, 128)]), [], ["gainraw"])
            TS("dve", gainrep, gainrep, 1.0 - lam_init, None, ALU.mult, None, ["gainraw"], ["gainrep"])
            DMA("sp", "p_cp", cps[0:34, :], cpar_d.ap()[layer], [], ["cps"])
            for cc in range(4):
                TR(ps[7][:, cc * 64:cc * 64 + 34], cps[0:34, cc * 128:(cc + 1) * 128], identf[0:34, 0:34],
                   ["cps", "identf"], [("ps", 7)])
            for cc in range(4):
                CP("dve", cp[:, cc, :], ps[7][:, cc * 64:cc * 64 + 34], [("ps", 7)], ["cp"])

            ev = 0
            for tt in (range(16) if layer == 0 else ()):
                xb = xin[tt % 4]
                DMA("sp" if tt % 2 == 0 else "act", "x0_%d" % (tt % 4), xb, xin_d.ap()[tt * 128:(tt + 1) * 128, :],
                    ["xres%d" % tt], [("xin", tt % 4)])
                for k4 in range(4):
                    bk = next_bank(0, 4)
                    for q in range(4):
                        kc = k4 * 4 + q
                        TR(ps[bk][:, q * 128:(q + 1) * 128], xb[:, kc * 128:(kc + 1) * 128], identf,
                           [("xin", tt % 4), "identf"], [("ps", bk)])
                    outv = big[:, k4 * 4:(k4 + 1) * 4, tt * 128:(tt + 1) * 128]
                    inv = ps[bk].rearrange("p (a b) -> p a b", b=128)
                    CP("act" if ev % 2 == 0 else "dve", outv, inv, [("ps", bk)],
                       [("big", k4 * 4 + q) for q in range(4)])
                    ev += 1
            for q_ in range(4):
                DMA("pool", "wobc", wob_d.ap()[q_ * 512:(q_ + 1) * 512, :],
                    wout_d.ap()[layer, q_ * 512:(q_ + 1) * 512, :], [], [("wob", q_)])
            BOUNDARY()

            al = A1()
            wst = [al.get(F32, 4, 512) for _ in range(2)]
            wb = [al.get(BF16, 16, 512) for _ in range(2)]
            stg = [al.get(F32, 512) for _ in range(4)]
            gstg = [al.get(BF16, 512) for _ in range(2)]
            stg_i = [0]
            gst_i = [0]
            wq_i = [0]

            def load_w(nblk, buf):
                for k4 in range(4):
                    q_ = wq_i[0] % 2
                    wq_i[0] += 1
                    src = DAP(win_d, layer * D * DIN + k4 * 4 * 128 * DIN + nblk * 512,
                              [(DIN, 128), (128 * DIN, 4), (1, 512)])
                    DMA("sp", "wst%d" % q_, wst[q_], src, [], [("wst", q_)])
                    for k_ in range(4):
                        kc = k4 * 4 + k_
                        CP("dve" if k_ % 2 == 0 else "pool", wb[buf][:, kc, :], wst[q_][:, k_, :],
                           [("wst", q_)], [("wb", buf, kc)])

            def store_stage(func, scale, dt, psb, bk, dst_ap, wkey, gain=False):
                si = stg_i[0] % 4
                stg_i[0] += 1
                sv = stg[si] if dt == F32 else stg[si].bitcast(BF16)[:, 0:512]
                if gain:
                    gi = gst_i[0] % 2
                    gst_i[0] += 1
                    ACTF(stg[si], psb, func, [("ps", bk)], [("st", si)], scale=scale)
                    TT("dve", gstg[gi].rearrange("p (a b) -> p a b", b=128),
                       stg[si].rearrange("p (a b) -> p a b", b=128), bc_mid(gainrep, 4), ALU.mult,
                       [("st", si), "gainrep"], [("gst", gi)])
                    DMA("act", "gst%d" % gi, dst_ap, gstg[gi], [("gst", gi)], [wkey])
                    return
                else:
                    ACTF(sv, psb, func, [("ps", bk)], [("st", si)], scale=scale)
                DMA("act", "st%d" % si, dst_ap, sv, [("st", si)], [wkey])

            BLK = {
                "aq": (0, AF.Copy, 0.125, True), "ak": (1, AF.Copy, 1.0, True),
                "av": (2, AF.Copy, 1.0, False), "ag": (3, AF.Silu, 1.0, False),
                "bq0": (4, AF.Copy, 128 ** -0.5, True), "bq1": (5, AF.Copy, 128 ** -0.5, True),
                "bk0": (6, AF.Copy, 1.0, True), "bk1": (7, AF.Copy, 1.0, True),
                "bv0": (8, AF.Copy, 1.0, False), "bv1": (9, AF.Copy, 1.0, False),
                "bg0": (10, AF.Silu, 1.0, True), "bg1": (11, AF.Silu, 1.0, True),
                "cu": (12, AF.Copy, 1.0, True), "cglu": (13, AF.Sigmoid, 1.0, True), "cg": (14, AF.Silu, 1.0, True),
            }
            ORDER = ["aq", "ak", "av", "ag", "bq0", "bk0", "bv0", "bg0", "bq1", "bk1", "bv1", "bg1", "cu", "cglu", "cg"]
            pj_state = {"i": 0}

            def gen_proj(count, banks=(6,)):
                i0 = pj_state["i"]
                if i0 == 0:
                    load_w(BLK[ORDER[0]][0], 0)
                for i_ in range(i0, i0 + count):
                    nm = ORDER[i_]
                    n, func, scale, fmaj = BLK[nm]
                    buf = i_ % 2
                    if i_ + 1 < len(ORDER):
                        load_w(BLK[ORDER[i_ + 1]][0], (i_ + 1) % 2)
                    for grp in range(16):
                        bk = banks[grp % len(banks)]
                        if fmaj:
                            fs, tb = grp // 4, grp % 4
                            for kc in range(16):
                                MM(ps[bk], wb[buf][:, kc, fs * 128:(fs + 1) * 128], big[:, kc, tb * 512:(tb + 1) * 512],
                                   kc == 0, kc == 15, [("wb", buf, kc), ("big", kc)], [("ps", bk)])
                                if kc % 4 == 3:
                                    yield
                            tsl = slice(tb * 512, (tb + 1) * 512)
                            if nm in ("aq", "ak"):
                                dst = qkA_d.ap()[fs, 0 if nm == "aq" else 1, :, tsl]
                                key = ("qkA", fs, nm, tb)
                                dt = BF16
                            elif nm[:2] in ("bq", "bk"):
                                hh = int(nm[2]) * 4 + fs
                                dst = qkB_d.ap()[hh, 0 if nm[:2] == "bq" else 1, :, tsl]
                                key = ("qkB", hh, nm[:2], tb)
                                dt = BF16
                            elif nm[:2] == "bg":
                                hh = int(nm[2]) * 4 + fs
                                dst = gBT_d.ap()[hh * 128:(hh + 1) * 128, tsl]
                                key = ("gBT", hh, tb)
                                dt = BF16
                            elif nm == "cu":
                                dst = cuT_d.ap()[fs * 128:(fs + 1) * 128, tsl]
                                key = ("cuT", fs, tb)
                                dt = F32
                            elif nm == "cglu":
                                dst = sgT_d.ap()[fs * 128:(fs + 1) * 128, tsl]
                                key = ("sgT", fs, tb)
                                dt = F32
                            else:
                                dst = cgT_d.ap()[fs * 128:(fs + 1) * 128, tsl]
                                key = ("cgT", fs, tb)
                                dt = BF16
                            store_stage(func, scale, dt, ps[bk], bk, dst, key)
                        else:
                            tt = grp
                            for kc in range(16):
                                MM(ps[bk], big[:, kc, tt * 128:(tt + 1) * 128], wb[buf][:, kc, :],
                                   kc == 0, kc == 15, [("wb", buf, kc), ("big", kc)], [("ps", bk)])
                                if kc % 4 == 3:
                                    yield
                            rsl = slice(tt * 128, (tt + 1) * 128)
                            if nm == "av":
                                dst = vA_d.ap()[rsl, :]
                                key = ("vA", tt)
                            elif nm == "ag":
                                dst = gA_d.ap()[rsl, :]
                                key = ("gA", tt)
                            else:
                                half = int(nm[2])
                                dst = vB_d.ap()[rsl, half * 512:(half + 1) * 512]
                                key = ("vB", half, tt)
                            store_stage(func, scale, BF16, ps[bk], bk, dst, key, gain=(nm == "ag"))
                pj_state["i"] = i0 + count

            def gen_A():
                al = A2()
                QT = [[al.get(BF16, S) for _ in range(2)] for _ in range(2)]
                KT = [al.get(BF16, S) for _ in range(2)]
                VA = [al.get(BF16, 16, 130) for _ in range(2)]
                RA = [al.get(F32, S) for _ in range(2)]
                GA = [al.get(BF16, 16, 128) for _ in range(2)]
                ssb = [al.get(F32, 256) for _ in range(3)]
                NPT = 8
                pT = [al.get(BF16, 256) for _ in range(8)]
                on0 = [al.get(F32, 2, 128) for _ in range(2)]
                rr = [al.get(F32, 16) for _ in range(2)]
                wa = [al.get(F32, 2, 128) for _ in range(2)]
                wsq = [al.get(F32, 2, 128) for _ in range(2)]
                wu = [al.get(F32, 2, 128) for _ in range(2)]
                wub = [al.get(BF16, 2, 128) for _ in range(2)]
                ost = [al.get(BF16, 256) for _ in range(2)]
                for b_ in range(2):
                    MEMSET("pool", VA[b_][:, :, 128:130], 1.0, [], [("VA", b_)])
                    MEMSET("pool", QT[b_][0][64:128, :], 0.0, [], [("QTz", b_, 0)])
                    MEMSET("pool", QT[b_][1][0:64, :], 0.0, [], [("QTz", b_, 1)])
                psT = ps[7].bitcast(BF16)
                pshA = [(0, ps[0]), (1, ps[1]), (6, ps[6])]
                LAG = 4

                def A_loads(h):
                    buf = h % 2
                    for c_ in range(2):
                        DMA("sp", "aQ%d%d" % (buf, c_), QT[buf][c_][c_ * 64:(c_ + 1) * 64, :],
                            qkA_d.ap()[h, 0, c_ * 64:(c_ + 1) * 64, :],
                            [("qkA", h, "aq", tb) for tb in range(4)], [("QT", buf, c_)])
                    DMA("sp", "aK%d" % buf, KT[buf], qkA_d.ap()[h, 1],
                        [("qkA", h, "ak", tb) for tb in range(4)], [("KT", buf)])
                    DMA("sp", "aV%d" % buf, VA[buf][:, :, 0:128],
                        DAP(vA_d, h * 128, [(512, 128), (128 * 512, 16), (1, 128)]),
                        [("vA", tt) for tt in range(16)], [("VA", buf)])
                    DMA("sp", "aR%d" % buf, RA[buf], biasA_d.ap()[h], [], [("RA", buf)])
                    DMA("sp", "aG%d" % buf, GA[buf],
                        DAP(gA_d, h * 128, [(512, 128), (128 * 512, 16), (1, 128)]),
                        [("gA", tt) for tt in range(16)], [("GA", buf)])

                stepsA = []
                gcount = 0
                for h in range(4):
                    for gg in range(8):
                        for c in range(2):
                            for j in range(2 * gg + 2):
                                stepsA.append((h, gg, c, j, gcount))
                            gcount += 1
                infoA = {}
                deferred = []

                def A_front(idx):
                    h, gg, c, j, gc = stepsA[idx]
                    buf = h % 2
                    a = max(0, j - 2 * gg)
                    q0 = gg * 256 + a * 128
                    N = 256 - a * 128
                    sbk, sap = pshA[idx % len(pshA)]
                    pb_ = idx % NPT
                    MM(sap[:, 0:N], KT[buf][:, j * 128:(j + 1) * 128],
                       QT[buf][c][:, q0:q0 + N], True, True,
                       [("KT", buf), ("QT", buf, c), ("QTz", buf, c)], [("ps", sbk)])
                    si3 = idx % 3
                    TT("dve", ssb[si3][:, 0:N], sap[:, 0:N], RA[buf][:, q0 - j * 128:q0 - j * 128 + N],
                       ALU.add, [("ps", sbk), ("RA", buf)], [("ssb", si3)])
                    ACTF(pT[pb_][:, 0:N], ssb[si3][:, 0:N], AF.Exp, [("ssb", si3)], [("pT", pb_)])
                    infoA[idx] = (a, pb_)

                def A_back(idx):
                    h, gg, c, j, gc = stepsA[idx]
                    buf = h % 2
                    a, pb_ = infoA[idx]
                    par = gc % 2
                    for i in range(2 * gg + a, 2 * gg + 2):
                        off = (i - 2 * gg - a) * 128
                        bk = 2 + 2 * par + (i - 2 * gg)
                        MM(ps[bk][:, 0:129], pT[pb_][:, off:off + 128], VA[buf][:, j, 0:129],
                           j == 0, j == i, [("pT", pb_), ("VA", buf)], [("ps", bk)])
                    if j == 2 * gg + 1:
                        A_post(h, gg, c, gc)

                def A_post(h, gg, c, gc):
                    buf = h % 2
                    par = gc % 2
                    b0 = 2 + 2 * par
                    acc3 = psall[:, b0 * 512:(b0 + 2) * 512].rearrange("p (a b) -> p a b", b=512)
                    accO = acc3[:, :, 0:128]
                    accl = acc3[:, :, 128:129]
                    pk = [("ps", b0), ("ps", b0 + 1)]
                    w_ = (gc // 2) % 2
                    R_ = rr[w_]
                    rk = ("rr", w_)
                    if c == 0:
                        RECIP(R_[:, 0:2].rearrange("p (a b) -> p a b", b=1), accl, pk, [(rk, 0)])
                        for a_ in range(2):
                            ACTF(on0[w_][:, a_, :], ps[b0 + a_][:, 0:128], AF.Copy, [("ps", b0 + a_), (rk, 0)],
                                 [("on0", w_, a_)], scale=R_[:, a_:a_ + 1])
                        return

                    def st1():
                        RECIP(R_[:, 2:4].rearrange("p (a b) -> p a b", b=1), accl, pk, [(rk, 1)])
                        TS("dve", R_[:, 4:6], R_[:, 2:4], neglam, None, ALU.mult, None, [(rk, 1), "neglam"], [(rk, 2)])
                        TT("dve", wa[w_], accO, bc_inner(R_[:, 4:6], 128), ALU.mult, pk + [(rk, 2)], [("wa", w_)])
                        TT("dve", wa[w_], wa[w_], on0[w_], ALU.add,
                           [("wa", w_), ("on0", w_, 0), ("on0", w_, 1)], [("wa", w_)])

                    def st2():
                        ACTF(wsq[w_], wa[w_], AF.Square, [("wa", w_)], [("wsq", w_)])
                        K.op("dve", lambda e, o=R_[:, 6:8], i_=wsq[w_]: e.reduce_sum(out=o, in_=i_, axis=AX.X),
                             [("wsq", w_)], [(rk, 3)])

                    def st3():
                        ACTF(R_[:, 8:10], R_[:, 6:8], AF.Ln, [(rk, 3), "epst"], [(rk, 4)], scale=1.0 / 128.0, bias=epst)
                        ACTF(R_[:, 10:12], R_[:, 8:10], AF.Exp, [(rk, 4)], [(rk, 5)], scale=-0.5)
                        for a_ in range(2):
                            ACTF(wu[w_][:, a_, :], wa[w_][:, a_, :], AF.Copy, [("wa", w_), (rk, 5)], [("wu", w_, a_)],
                                 scale=R_[:, 10 + a_:11 + a_])

                    def st4():
                        TT("dve", wub[w_], wu[w_], GA[buf][:, 2 * gg:2 * gg + 2, :], ALU.mult,
                           [("wu", w_, 0), ("wu", w_, 1), ("GA", buf)], [("wub", w_)])

                    def st5():
                        for a_ in range(2):
                            TR(psT[:, w_ * 256 + a_ * 128:w_ * 256 + (a_ + 1) * 128], wub[w_][:, a_, :], identb,
                               [("wub", w_), "identb"], [("ps", 7)])
                        CP("act", ost[w_], psT[:, w_ * 256:(w_ + 1) * 256], [("ps", 7)], [("ost", w_)])
                        DMA("pool", "aO%d" % w_, catT_d.ap()[h * 128:(h + 1) * 128, gg * 256:(gg + 1) * 256], ost[w_],
                            [("ost", w_)], [("catT", h, gg)])

                    pid = gc // 2
                    while deferred and deferred[0][0] <= pid - 2:
                        f_ = deferred.pop(0)[1]
                        if f_ is not None:
                            f_()
                    st1()
                    deferred.extend([(pid, st2), (pid, st3), (pid, st4)] + [(pid, None)] * 5 + [(pid, st5)])

                A_loads(0)
                A_loads(1)
                nA = len(stepsA)
                for idx in range(nA):
                    A_front(idx)
                    if idx >= LAG:
                        A_back(idx - LAG)
                        hp = stepsA[idx - LAG][0]
                        if stepsA[idx - LAG + 1][0] != hp and hp + 2 < 4:
                            while deferred:
                                f_ = deferred.pop(0)[1]
                                if f_ is not None:
                                    f_()
                            A_loads(hp + 2)
                    if deferred:
                        f_ = deferred.pop(0)[1]
                        if f_ is not None:
                            f_()
                    yield
                for k_ in range(LAG, 0, -1):
                    A_back(nA - k_)
                while deferred:
                    f_ = deferred.pop(0)[1]
                    if f_ is not None:
                        f_()

            def gen_B():
                al = A12()
                QN = [al.get(BF16, S) for _ in range(2)]
                KN = [al.get(BF16, S) for _ in range(2)]
                VP = [al.get(BF16, 3, 16, 128) for _ in range(2)]
                B2 = [al.get(F32, 3, 256) for _ in range(2)]
                GT = [al.get(BF16, S) for _ in range(2)]
                PT = [al.get(BF16, 48, 256) for _ in range(2)]
                QP = [[al.get(BF16, S) for _ in range(2)] for _ in range(2)]
                ssB = [al.get(F32, 256) for _ in range(3)]
                rlb = [al.get(F32, 512) for _ in range(2)]
                o32 = [al.get(F32, 512) for _ in range(2)]
                bst = [al.get(BF16, S) for _ in range(2)]
                SBK = (0, 1, 6)

                def B_loads(h):
                    buf = h % 2
                    DMA("sp", "bQ%d" % buf, QN[buf], qkB_d.ap()[h, 0],
                        [("qkB", h, "bq", tb) for tb in range(4)], [("QN", buf)])
                    DMA("sp", "bK%d" % buf, KN[buf], qkB_d.ap()[h, 1],
                        [("qkB", h, "bk", tb) for tb in range(4)], [("KN", buf)])
                    for p, (win, dil) in enumerate(PATTERNS):
                        Lp = S // dil
                        nb = Lp // 128
                        for r in range(dil):
                            src = DAP(vB_d, r * 1024 + h * 128, [(dil * 1024, 128), (128 * dil * 1024, nb), (1, 128)])
                            last = (p == 2 and r == dil - 1)
                            DMA("sp", "bV%d" % buf, VP[buf][:, p, r * nb:(r + 1) * nb, :], src,
                                [("vB", h // 4, tt) for tt in range(16)],
                                [("VP", buf, p, r)] + ([("VPall", buf)] if last else []))
                    DMA("sp", "bB%d" % buf, B2[buf], DAP(biasB_d, h * 128 * 256, [(256, 128), (8 * 128 * 256, 3), (1, 256)]),
                        [], [("B2", buf)])
                    DMA("sp", "bG%d" % buf, GT[buf], gBT_d.ap()[h * 128:(h + 1) * 128, :],
                        [("gBT", h, tb) for tb in range(4)], [("GT", buf)])
                    for pi, dil in ((0, 4), (1, 16)):
                        CP("pool", QP[buf][pi].rearrange("p (r l) -> p r l", r=dil),
                           QN[buf].rearrange("p (l r) -> p r l", r=dil), [("QN", buf)], [("QP", buf, pi)])

                def tid_p0(kb):
                    return 16 + (kb // 4) * 8 + kb % 4

                def tid_p1(r, b):
                    return 16 + b * 8 + 4 + r

                fronts = []
                group_last = {}
                for h in range(8):
                    for r16 in range(16):
                        fronts.append((h, r16, 2, r16, 16, 128))
                    for TB in range(4):
                        for kb in range(4 * TB, 4 * TB + 4):
                            fronts.append((h, tid_p0(kb), 0, kb * 128, 1, 256 if kb < 15 else 128))
                        for r in range(4):
                            fronts.append((h, tid_p1(r, TB), 1, TB * 512 + r, 4, 256 if TB < 3 else 128))
                        group_last[(h, TB)] = len(fronts) - 1
                fctr = [0]

                def B_front(fi):
                    h, tid, p, t0, dil, nq = fronts[fi]
                    buf = h % 2
                    sbk = SBK[fi % 3]
                    si3 = fi % 3
                    kap = KN[buf][:, t0:t0 + 127 * dil + 1:dil]
                    if p == 0:
                        qap = QN[buf][:, t0:t0 + nq]
                        qk = ("QN", buf)
                    else:
                        r_, l0 = t0 % dil, t0 // dil
                        qap = QP[buf][p - 1][:, r_ * (S // dil) + l0:r_ * (S // dil) + l0 + nq]
                        qk = ("QP", buf, p - 1)
                    MM(ps[sbk][:, 0:nq], kap, qap, True, True, [qk, ("KN", buf)], [("ps", sbk)])
                    TT("dve", ssB[si3][:, 0:nq], ps[sbk][:, 0:nq], B2[buf][:, p, 0:nq], ALU.add,
                       [("ps", sbk), ("B2", buf)], [("ssB", si3)])
                    ACTF(PT[buf][:, tid, 0:nq], ssB[si3][:, 0:nq], AF.Exp, [("ssB", si3)], [("PT", buf, tid)])

                blk_ctr = [0]

                def B_block(h, TB):
                    buf = h % 2
                    par = blk_ctr[0] % 2
                    blk_ctr[0] += 1
                    bO, bL = 2 + par, 4 + par
                    contrib = []
                    for qb in range(4 * TB, 4 * TB + 4):
                        oc = ((qb - 4 * TB) * 128, 1, 128)
                        if qb - 1 >= 0:
                            contrib.append((tid_p0(qb - 1), 128, 128, oc, VP[buf][:, 0, qb - 1, :], ("VP", buf, 0, 0)))
                        contrib.append((tid_p0(qb), 0, 128, oc, VP[buf][:, 0, qb, :], ("VP", buf, 0, 0)))
                    for r in range(4):
                        oc = (r, 4, 128)
                        if TB - 1 >= 0:
                            contrib.append((tid_p1(r, TB - 1), 128, 128, oc, VP[buf][:, 1, r * 4 + TB - 1, :], ("VP", buf, 1, r)))
                        contrib.append((tid_p1(r, TB), 0, 128, oc, VP[buf][:, 1, r * 4 + TB, :], ("VP", buf, 1, r)))
                    for r16 in range(16):
                        contrib.append((r16, 32 * TB, 32, (r16, 16, 32), VP[buf][:, 2, r16, :], ("VP", buf, 2, r16)))
                    n = len(contrib)

                    def mk(k_, tid, c0, ncol, o0, ostep, ocnt, vt, vkey):
                        def f():
                            rhs = PT[buf][:, tid, c0:c0 + ncol]
                            osl = slice(o0, o0 + (ocnt - 1) * ostep + 1, ostep)
                            MM(ps[bO][:, osl], vt, rhs, k_ == 0, k_ == n - 1,
                               [vkey, ("VPall", buf), ("PT", buf, tid)], [("ps", bO)])
                            MM(ps[bL][:, osl], onesb, rhs, k_ == 0, k_ == n - 1,
                               ["onesb", ("PT", buf, tid)], [("ps", bL)])
                        return f

                    for k_, (tid, c0, ncol, (o0, ostep, ocnt), vt, vkey) in enumerate(contrib):
                        pvq.append(mk(k_, tid, c0, ncol, o0, ostep, ocnt, vt, vkey))

                    def fin():
                        w_ = par
                        tsl = slice(TB * 512, (TB + 1) * 512)
                        ACTF(rlb[w_], ps[bL], AF.Ln, [("ps", bL)], [("rlb", w_)])
                        ACTF(rlb[w_], rlb[w_], AF.Exp, [("rlb", w_)], [("rlb", w_)], scale=-1.0)
                        TT("dve", o32[w_], ps[bO], rlb[w_], ALU.mult, [("ps", bO), ("rlb", w_)], [("o32", w_)])
                        TT("pool", bst[buf][:, tsl], o32[w_], GT[buf][:, tsl], ALU.mult,
                           [("o32", w_), ("GT", buf)], [("bst", buf, TB)])
                        if TB == 3:
                            DMA("pool", "bO%d" % buf, catT_d.ap()[(4 + h) * 128:(5 + h) * 128, :], bst[buf],
                                [("bst", buf, t_) for t_ in range(4)], [("catT", 4 + h)])
                            if h + 2 < 8:
                                B_loads(h + 2)
                    pvq.append(fin)

                LAGF = 4
                pvq = []
                pending = []
                for h in range(8):
                    for TB in range(4):
                        pending.append((group_last[(h, TB)], h, TB))
                B_loads(0)
                B_loads(1)
                for fi in range(len(fronts)):
                    B_front(fi)
                    while pending and pending[0][0] + LAGF <= fi:
                        _, h_, TB_ = pending.pop(0)
                        B_block(h_, TB_)
                    for _k in range(3):
                        if pvq:
                            pvq.pop(0)()
                    yield
                while pending:
                    _, h_, TB_ = pending.pop(0)
                    B_block(h_, TB_)
                while pvq:
                    pvq.pop(0)()

            drain(gen_proj(15, (4, 5, 6, 7)), ("R1",))
            BOUNDARY(("R2",))
            drain(gen_A(), ("R2",))
            BOUNDARY()
            drain(gen_B(), ("R2",))
            BOUNDARY()
            K.tags = TALL

            for hh in range(12):
                DMA("sp", "catl", big[:, hh, :], catT_d.ap()[hh * 128:(hh + 1) * 128, :], ([("catT", hh, g_) for g_ in range(8)] if hh < 4 else [("catT", hh)]), [("big", hh)])
            al = A12()
            wpst = al.get(F32, 4, 512)
            wpw = al.get(BF16, 4, 512)
            Dg = al.get(BF16, 4, 31, 128)
            cu = al.get(F32, S)
            sg = al.get(F32, S)
            hpad = al.get(BF16, 4, S + 32)
            y1 = al.get(F32, 4, S)
            ysq = wpst
            mean = al.get(F32, 512)
            msq = al.get(F32, 512)
            rstd = al.get(F32, 512)
            hn = [al.get(F32, 512) for _ in range(2)]
            hs = al.get(BF16, 4, 512)
            cgt = [al.get(BF16, 512) for _ in range(2)]
            DMA("sp", "c_w", wpst, DAP(wpw_d, layer * 512 * 512, [(512, 128), (128 * 512, 4), (1, 512)]), [], ["wpst"])
            CP("pool", wpw, wpst, ["wpst"], ["wpw"])
            for cc in range(4):
                for j in range(31):
                    if j % 2 == 0:
                        TS("dve", Dg[:, cc, j, :], identb, cp[:, cc, j:j + 1], None, ALU.mult, None,
                           ["identb", "cp"], [("Dg", cc, j)])
                    else:
                        ACTF(Dg[:, cc, j, :], identb, AF.Copy, ["identb", "cp"], [("Dg", cc, j)],
                             scale=cp[:, cc, j:j + 1])
            for cc in range(4):
                DMA("sp", "c_cu", cu, cuT_d.ap()[cc * 128:(cc + 1) * 128, :], [("cuT", cc, tb) for tb in range(4)], ["cu"])
                DMA("act", "c_sg", sg, sgT_d.ap()[cc * 128:(cc + 1) * 128, :], [("sgT", cc, tb) for tb in range(4)], ["sg"])
                MEMSET("pool", hpad[:, cc, 0:30], 0.0, [], [("hpad0", cc)])
                TT("dve", hpad[:, cc, 30:30 + S], cu, sg, ALU.mult, ["cu", "sg"], [("hpad", cc)])
            for cc in range(4):
                for tb in range(4):
                    bk = next_bank(0, 4)
                    for j in range(31):
                        MM(ps[bk], Dg[:, cc, j, :], hpad[:, cc, tb * 512 + j:tb * 512 + j + 512], j == 0, j == 30,
                           [("Dg", cc, j), ("hpad", cc), ("hpad0", cc)], [("ps", bk)])
                    ACTF(y1[:, cc, tb * 512:(tb + 1) * 512], ps[bk], AF.Identity, [("ps", bk), "cp"], [("y1", cc, tb)],
                         bias=cp[:, cc, 31:32])
            for tb in range(4):
                tsl = slice(tb * 512, (tb + 1) * 512)
                for cc in range(4):
                    ACTF(ysq[:, cc, :], y1[:, cc, tsl], AF.Square, [("y1", cc, tb)], [("ysq", cc), "wpst"])
                for cc in range(4):
                    MM(ps[4], onesf, y1[:, cc, tsl], cc == 0, cc == 3, ["onesf", ("y1", cc, tb)], [("ps", 4)])
                for cc in range(4):
                    MM(ps[5], onesf, ysq[:, cc, :], cc == 0, cc == 3, ["onesf", ("ysq", cc)], [("ps", 5)])
                ACTF(mean, ps[4], AF.Copy, [("ps", 4)], ["mean"], scale=1.0 / 512.0)
                TT("dve", msq, mean, mean, ALU.mult, ["mean"], ["msq"])
                STT(rstd, ps[5], 1.0 / 512.0, msq, ALU.mult, ALU.subtract, [("ps", 5), "msq"], ["rstd"])
                ACTF(rstd, rstd, AF.Ln, ["rstd", "epst"], ["rstd"], bias=epst)
                ACTF(rstd, rstd, AF.Exp, ["rstd"], ["rstd"], scale=-0.5)
                for cc in range(4):
                    hb = hn[cc % 2]
                    TT("dve", hb, y1[:, cc, tsl], mean, ALU.subtract, [("y1", cc, tb), "mean"], [("hn", cc % 2)])
                    TT("dve", hb, hb, rstd, ALU.mult, [("hn", cc % 2), "rstd"], [("hn", cc % 2)])
                    ACTF(hs[:, cc, :], hb, AF.Silu, [("hn", cc % 2), "cp"], [("hs", cc)],
                         scale=cp[:, cc, 32:33], bias=cp[:, cc, 33:34])
                for co in range(4):
                    bk = next_bank(0, 4)
                    gb = cgt[co % 2]
                    DMA("sp", "c_g%d" % (co % 2), gb, cgT_d.ap()[co * 128:(co + 1) * 128, tsl],
                        [("cgT", co, tb)], [("cgt", co % 2)])
                    for cc in range(4):
                        MM(ps[bk], wpw[:, cc, co * 128:(co + 1) * 128], hs[:, cc, :], cc == 0, cc == 3,
                           ["wpw", ("hs", cc)], [("ps", bk)])
                    TT("dve", big[:, 12 + co, tsl], ps[bk], gb, ALU.mult, [("ps", bk), ("cgt", co % 2)],
                       [("big", 12 + co)])

            BOUNDARY()
            if ("cat%d" % layer) in tap_d:
                DMA("sp", "tapcat", DAP(tap_d["cat%d" % layer], 0, [(S, 128), (128 * S, 16), (1, S)]), big,
                    [("big", kc) for kc in range(16)], ["tapcat"])
                BOUNDARY()
            al = A12()
            wo = al.get(BF16, 16, D)
            zt = [al.get(F32, D) for _ in range(2)]
            xrs = [al.get(F32, D) for _ in range(2)]
            grep = al.get(F32, D)
            brep = al.get(F32, D)
            stats = al.get(F32, 24)
            mv = al.get(F32, 4)
            DMA("act", "o_g", grep, DAP(lng_d, layer * D, [(0, 128), (1, D)]), [], ["grep"])
            DMA("act", "o_b", brep, DAP(lnb_d, layer * D, [(0, 128), (1, D)]), [], ["brep"])
            for n in range(4):
                DMA("sp", "wo4_%d" % n, wo[:, :, n * 512:(n + 1) * 512],
                    DAP(wob_d, n * 512, [(D, 128), (128 * D, 16), (1, 512)]),
                    [("wob", q_) for q_ in range(4)], [("wo", kc, n) for kc in range(16)])

            def load_x(tt):
                DMA("sp", "o_x%d" % (tt % 2), xrs[tt % 2], xin_d.ap()[tt * 128:(tt + 1) * 128, :],
                    ["xres%d" % tt], [("xr", tt % 2)])

            def next_xT(tt_):
                zb_ = zt[tt_ % 2]
                Zq_ = [("z", tt_ % 2, n_) for n_ in range(4)]
                for k4 in range(4):
                    bk_ = next_bank(0, 8)
                    for q in range(4):
                        kc_ = k4 * 4 + q
                        TR(ps[bk_][:, q * 128:(q + 1) * 128], zb_[:, kc_ * 128:(kc_ + 1) * 128], identf,
                           Zq_ + ["identf"], [("ps", bk_)])
                    CP("act", big[:, k4 * 4:(k4 + 1) * 4, tt_ * 128:(tt_ + 1) * 128],
                       ps[bk_].rearrange("p (a b) -> p a b", b=128), [("ps", bk_)], [("bigx", k4, tt_)])

            fuse_next = layer + 1 < n_layers
            load_x(0)
            for tt in range(16):
                zb = zt[tt % 2]
                xr = xrs[tt % 2]
                rsl = slice(tt * 128, (tt + 1) * 128)
                if tt + 1 < 16:
                    load_x(tt + 1)
                for n in range(4):
                    bk = next_bank(0, 8)
                    for kc in range(16):
                        MM(ps[bk], big[:, kc, rsl], wo[:, kc, n * 512:(n + 1) * 512], kc == 0, kc == 15,
                           [("big", kc), ("wo", kc, n)], [("ps", bk)])
                    STT(zb[:, n * 512:(n + 1) * 512], xr[:, n * 512:(n + 1) * 512], ALPHA, ps[bk], ALU.mult, ALU.add,
                        [("xr", tt % 2), ("ps", bk)], [("z", tt % 2, n)])
                if fuse_next and tt >= 1:
                    next_xT(tt - 1)
                for n in range(4):
                    K.op("dve", lambda e, o=stats[:, n * 6:(n + 1) * 6], i_=zb[:, n * 512:(n + 1) * 512]: e.bn_stats(out=o, in_=i_),
                         [("z", tt % 2, n)], [("stats", n)])
                K.op("dve", lambda e, o=mv[:, 0:2], i_=stats: e.bn_aggr(out=o, in_=i_),
                     [("stats", n) for n in range(4)], ["mv"])
                ACTF(mv[:, 2:3], mv[:, 1:2], AF.Sqrt, ["mv", "epst"], ["mv2"], bias=epst)
                RECIP(mv[:, 3:4], mv[:, 2:3], ["mv2"], ["mv3"])
                Zq = [("z", tt % 2, n) for n in range(4)]
                TS("dve", zb, zb, mv[:, 0:1], mv[:, 3:4], ALU.subtract, ALU.mult, Zq + ["mv", "mv3"], Zq)
                TT("pool", zb, zb, grep, ALU.mult, Zq + ["grep"], Zq)
                TT("pool", zb, zb, brep, ALU.add, Zq + ["brep"], Zq)
                DMA("pool", "o_y%d" % (tt % 2), xout_d.ap()[rsl, :], zb, Zq, ["xres%d" % tt])
            if fuse_next:
                next_xT(15)
        K.emit(st)
    return nc


def _host_tables(rel_bias):
    rel_bias = np.asarray(rel_bias, dtype=np.float32)
    kl = np.arange(128)[:, None]
    m = np.arange(S)[None, :]
    dist = m - kl
    bk = t5_bucket_np(np.clip(dist, 0, S - 1))
    biasA = np.empty((4, 128, S), np.float32)
    for h in range(4):
        biasA[h] = np.where(dist >= 0, rel_bias[bk, h], np.float32(MASK))
    biasB = np.empty((3, 8, 128, 256), np.float32)
    kj = np.arange(128)[:, None]
    qq = np.arange(256)[None, :]
    lag = qq - kj
    valid = (lag >= 0) & (lag <= 128)
    for p, (win, dil) in enumerate(PATTERNS):
        bkt = t5_bucket_np(np.clip(lag, 0, 255) * dil)
        for h in range(8):
            biasB[p, h] = np.where(valid, rel_bias[bkt, 4 + h], np.float32(MASK))
    return biasA, biasB


_NC_CACHE = {}


def kernel(x, w_in, diff_lambda, diff_head_gain, conv_dw, conv_b, conv_ln_g, conv_ln_b,
           conv_pw, w_out, ln_g, ln_b, rel_bias):
    f = lambda a: np.ascontiguousarray(np.asarray(a, dtype=np.float32))
    x = f(x)
    biasA, biasB = _host_tables(rel_bias)
    cpar = np.concatenate([f(conv_dw), f(conv_b)[:, None, :], f(conv_ln_g)[:, None, :], f(conv_ln_b)[:, None, :]],
                          axis=1)
    common = {
        "w_in": f(w_in), "w_out": f(w_out), "conv_pw": f(conv_pw),
        "dlam": f(diff_lambda).reshape(DEPTH, 256), "hgain": f(diff_head_gain),
        "cpar": np.ascontiguousarray(cpar), "ln_g": f(ln_g), "ln_b": f(ln_b),
        "biasA": biasA, "biasB": biasB, "ident": np.eye(128, dtype=np.float32),
    }
    if "nc" not in _NC_CACHE:
        _NC_CACHE["nc"] = build()
    nc = _NC_CACHE["nc"]
    in_maps = [dict(common, x=x[c]) for c in range(8)]
    res = run_bass_kernel_spmd(nc, in_maps, core_ids=list(range(8)))
    return np.stack([np.asarray(r["y"], dtype=np.float32) for r in res.results], axis=0)
```
